# Optimizing a Trainium2 kernel written in Bass

```python
import jax, jax.numpy as jnp
from jax import lax
import numpy as np

D_MODEL = 2048
BATCH = 8
SEQ = 4096
DEPTH = 4

N_MIXERS = 2
MEM_TOKENS = 256
BRANCH_WIDTH = 2 * D_MODEL
POOL_WINDOWS = (2, 4, 8, 16)
N_POOL_GROUPS = len(POOL_WINDOWS)
POOL_GROUP_WIDTH = BRANCH_WIDTH // N_POOL_GROUPS
DN_HEAD_DIM = 128
DN_V_HEADS = BRANCH_WIDTH // DN_HEAD_DIM
DN_QK_HEADS = DN_V_HEADS // 2
DN_KEY_WIDTH = DN_QK_HEADS * DN_HEAD_DIM
DN_CONV_WIDTH = 4
DN_CONV_CHANNELS = 2 * DN_KEY_WIDTH + BRANCH_WIDTH
DN_CHUNK = 64
XA_HEADS = 4
XA_WIDTH = D_MODEL
XA_HEAD_DIM = XA_WIDTH // XA_HEADS
MIX_WIDTH = BRANCH_WIDTH + XA_WIDTH
POOL_IN_WIDTH = BRANCH_WIDTH + MIX_WIDTH + XA_WIDTH
DELTA_IN_WIDTH = DN_CONV_CHANNELS + MIX_WIDTH + XA_WIDTH + 2 * DN_V_HEADS
N_POOL_LAYERS = (DEPTH + 1) // 2
N_DELTA_LAYERS = DEPTH // 2
EPS = 1e-6

kernel_name = "hybrid_pool_deltanet_memxattn_trunk"


def rms_norm(x, g):
    xf = x.astype(jnp.float32)
    y = xf * lax.rsqrt(jnp.mean(xf * xf, axis=-1, keepdims=True) + EPS)
    return (y * g.astype(jnp.float32)).astype(x.dtype)


def l2_normalize(x):
    return x * lax.rsqrt(jnp.sum(x * x, axis=-1, keepdims=True) + EPS)


def multiscale_pool_mixer(u, group_maps, scale):
    b, s, _ = u.shape
    uf = u.astype(jnp.float32).reshape(b, s, N_POOL_GROUPS, POOL_GROUP_WIDTH)
    csum = jnp.cumsum(uf, axis=1)
    pos = jnp.arange(1, s + 1, dtype=jnp.float32)
    diffs = []
    for gi, w in enumerate(POOL_WINDOWS):
        c = csum[:, :, gi]
        lagged = jnp.pad(c[:, : s - w], ((0, 0), (w, 0), (0, 0)))
        mean = (c - lagged) / jnp.minimum(pos, float(w))[None, :, None]
        diffs.append(mean - uf[:, :, gi])
    d = jnp.stack(diffs, axis=2).astype(u.dtype)
    y = jnp.einsum('bsgc,gcd->bsgd', d, group_maps).reshape(b, s, BRANCH_WIDTH)
    return y * scale


def causal_depthwise_conv(u, w):
    k = w.shape[0]
    return lax.conv_general_dilated(
        u, w[:, None, :].astype(u.dtype), window_strides=(1,), padding=[(k - 1, 0)],
        dimension_numbers=('NWC', 'WIO', 'NWC'), feature_group_count=u.shape[-1])


def chunk_gated_delta_rule(q, k, v, g, beta):
    b, s, h, dk = q.shape
    dv = v.shape[-1]
    c = DN_CHUNK
    n = s // c

    def to_chunks(t):
        return jnp.moveaxis(t.reshape((b, n, c, h) + t.shape[3:]), 3, 1)

    q = to_chunks(q) * (dk ** -0.5)
    k = to_chunks(k)
    v = to_chunks(v)
    beta = to_chunks(beta)
    g = jnp.cumsum(to_chunks(g), axis=-1)
    causal = jnp.tril(jnp.ones((c, c), dtype=bool))
    strict = jnp.tril(jnp.ones((c, c), dtype=bool), -1)
    diff = g[..., :, None] - g[..., None, :]
    decay = jnp.where(causal, jnp.exp(jnp.where(causal, diff, 0.0)), 0.0)
    k_beta = k * beta[..., None]
    lower = jnp.where(strict, jnp.einsum('bhncd,bhnmd->bhncm', k_beta, k) * decay, 0.0)
    eye = jnp.eye(c, dtype=jnp.float32)
    rhs = jnp.concatenate([v * beta[..., None], k_beta * jnp.exp(g)[..., None]], axis=-1)
    sol = lax.linalg.triangular_solve(eye + lower, rhs, left_side=True, lower=True, unit_diagonal=True)
    u_c, w_c = sol[..., :dv], sol[..., dv:]
    qk = jnp.where(causal, jnp.einsum('bhncd,bhnmd->bhncm', q, k) * decay, 0.0)
    g_last = g[..., -1]
    q_dec = q * jnp.exp(g)[..., None]
    k_dec = k * jnp.exp(g_last[..., None] - g)[..., None]

    def step(state, xs):
        q_i, k_i, qk_i, u_i, w_i, gl_i = xs
        v_new = u_i - jnp.einsum('bhcd,bhde->bhce', w_i, state)
        out = jnp.einsum('bhcd,bhde->bhce', q_i, state) + jnp.einsum('bhcm,bhme->bhce', qk_i, v_new)
        state = state * jnp.exp(gl_i)[..., None, None] + jnp.einsum('bhcd,bhce->bhde', k_i, v_new)
        return state, out

    xs = tuple(jnp.moveaxis(t, 2, 0) for t in (q_dec, k_dec, qk, u_c, w_c, g_last))
    state0 = jnp.zeros((b, h, dk, dv), jnp.float32)
    _, out = lax.scan(step, state0, xs)
    return jnp.transpose(out, (1, 0, 3, 2, 4)).reshape(b, s, h, dv)


def gated_deltanet_branch(qkv_in, b_logit, a_logit, conv_w, a_log, dt_bias, norm_g):
    bsz, s, _ = qkv_in.shape
    qkv = jax.nn.silu(causal_depthwise_conv(qkv_in, conv_w))
    q, k, v = jnp.split(qkv, [DN_KEY_WIDTH, 2 * DN_KEY_WIDTH], axis=-1)
    rep = DN_V_HEADS // DN_QK_HEADS
    q = jnp.repeat(l2_normalize(q.astype(jnp.float32).reshape(bsz, s, DN_QK_HEADS, DN_HEAD_DIM)), rep, axis=2)
    k = jnp.repeat(l2_normalize(k.astype(jnp.float32).reshape(bsz, s, DN_QK_HEADS, DN_HEAD_DIM)), rep, axis=2)
    v = v.astype(jnp.float32).reshape(bsz, s, DN_V_HEADS, DN_HEAD_DIM)
    beta = jax.nn.sigmoid(b_logit.astype(jnp.float32))
    g = -jnp.exp(a_log.astype(jnp.float32)) * jax.nn.softplus(a_logit.astype(jnp.float32) + dt_bias.astype(jnp.float32))
    o = chunk_gated_delta_rule(q, k, v, g, beta)
    o = rms_norm(o, norm_g)
    return o.reshape(bsz, s, BRANCH_WIDTH).astype(qkv_in.dtype)


def memory_cross_attention(q, mem_n, w_kv):
    b, s, _ = q.shape
    k, v = jnp.split(mem_n @ w_kv, 2, axis=-1)
    q = q.reshape(b, s, XA_HEADS, XA_HEAD_DIM)
    k = k.reshape(b, -1, XA_HEADS, XA_HEAD_DIM)
    v = v.reshape(b, -1, XA_HEADS, XA_HEAD_DIM)
    scores = jnp.einsum('bshd,bmhd->bhsm', q, k).astype(jnp.float32) * (XA_HEAD_DIM ** -0.5)
    p = jax.nn.softmax(scores, axis=-1).astype(v.dtype)
    return jnp.einsum('bhsm,bmhd->bshd', p, v).reshape(b, s, XA_WIDTH)


def setup_inputs(seed: int = 0) -> dict:
    key = jax.random.key(seed)
    ks = jax.random.split(key, 16)
    f32 = jnp.float32

    def normal(k, shape, scale):
        return jax.random.normal(k, shape, f32) * scale

    x = normal(ks[0], (BATCH, SEQ, D_MODEL), 1.0)
    mem = normal(ks[1], (BATCH, MEM_TOKENS, D_MODEL), 1.0)
    layer_norm_g = 1.0 + normal(ks[2], (DEPTH, D_MODEL), 0.02)
    mem_norm_g = 1.0 + normal(ks[3], (D_MODEL,), 0.02)
    final_norm_g = 1.0 + normal(ks[4], (D_MODEL,), 0.02)
    w_in_pool = normal(ks[5], (N_POOL_LAYERS, D_MODEL, POOL_IN_WIDTH), D_MODEL ** -0.5)
    pool_maps = normal(ks[6], (N_POOL_LAYERS, N_POOL_GROUPS, POOL_GROUP_WIDTH, POOL_GROUP_WIDTH), POOL_GROUP_WIDTH ** -0.5)
    pool_scale = 1.0 + normal(ks[7], (N_POOL_LAYERS, BRANCH_WIDTH), 0.02)
    w_in_delta = normal(ks[8], (N_DELTA_LAYERS, D_MODEL, DELTA_IN_WIDTH), D_MODEL ** -0.5)
    dn_conv_w = normal(ks[9], (N_DELTA_LAYERS, DN_CONV_WIDTH, DN_CONV_CHANNELS), DN_CONV_WIDTH ** -0.5)
    dn_a_log = jnp.log(jax.random.uniform(ks[10], (N_DELTA_LAYERS, DN_V_HEADS), f32, 1.0, 16.0))
    dt = jnp.exp(jax.random.uniform(ks[11], (N_DELTA_LAYERS, DN_V_HEADS), f32, np.log(1e-3), np.log(1e-1)))
    dn_dt_bias = dt + jnp.log(-jnp.expm1(-dt))
    dn_norm_g = 1.0 + normal(ks[12], (N_DELTA_LAYERS, DN_HEAD_DIM), 0.02)
    w_mem_kv = normal(ks[13], (DEPTH, D_MODEL, 2 * XA_WIDTH), D_MODEL ** -0.5)
    w_out = normal(ks[14], (DEPTH, MIX_WIDTH, D_MODEL), MIX_WIDTH ** -0.5)
    return {"x": x, "mem": mem, "layer_norm_g": layer_norm_g, "mem_norm_g": mem_norm_g,
            "final_norm_g": final_norm_g, "w_in_pool": w_in_pool, "pool_maps": pool_maps,
            "pool_scale": pool_scale, "w_in_delta": w_in_delta, "dn_conv_w": dn_conv_w,
            "dn_a_log": dn_a_log, "dn_dt_bias": dn_dt_bias, "dn_norm_g": dn_norm_g,
            "w_mem_kv": w_mem_kv, "w_out": w_out}


def reference(x, mem, layer_norm_g, mem_norm_g, final_norm_g, w_in_pool, pool_maps, pool_scale,
              w_in_delta, dn_conv_w, dn_a_log, dn_dt_bias, dn_norm_g, w_mem_kv, w_out):
    mem_n = rms_norm(mem, mem_norm_g)
    h = x
    d_z = DN_CONV_CHANNELS + MIX_WIDTH
    d_q = d_z + XA_WIDTH
    d_b = d_q + DN_V_HEADS
    for layer in range(DEPTH):
        j = layer // N_MIXERS
        xn = rms_norm(h, layer_norm_g[layer])
        if layer % N_MIXERS == 0:
            proj = xn @ w_in_pool[j]
            u, z, q_mem = jnp.split(proj, [BRANCH_WIDTH, BRANCH_WIDTH + MIX_WIDTH], axis=-1)
            branch = multiscale_pool_mixer(u, pool_maps[j], pool_scale[j])
        else:
            proj = xn @ w_in_delta[j]
            qkv_in, z, q_mem, b_logit, a_logit = jnp.split(proj, [DN_CONV_CHANNELS, d_z, d_q, d_b], axis=-1)
            branch = gated_deltanet_branch(qkv_in, b_logit, a_logit, dn_conv_w[j], dn_a_log[j],
                                           dn_dt_bias[j], dn_norm_g[j])
        mem_out = memory_cross_attention(q_mem, mem_n, w_mem_kv[layer])
        mixed = jnp.concatenate([branch, mem_out], axis=-1) * jax.nn.silu(z)
        h = h + mixed @ w_out[layer]
    return rms_norm(h, final_norm_g)
```

```python
import os
import numpy as np
import concourse.bass as bass
import concourse.mybir as mybir
from concourse.bass_utils import run_bass_kernel_spmd
from contextlib import ExitStack

F32 = mybir.dt.float32
BF16 = mybir.dt.bfloat16
AF = mybir.ActivationFunctionType
ALU = mybir.AluOpType
AX = mybir.AxisListType

D = 2048
NKC = 16
MEM = 256
BR = 4096
MIX = 6144
XA = 2048
POOL_IN = 12288
DELTA_IN = 16448
EPS = 1e-6
TT = 512
SEM_LIMIT = 30000
POOLENG = os.environ.get('KPOOL', 'dve')


class Op:
    __slots__ = ("eng", "fn", "sem", "inc", "deps", "val", "epoch", "marked", "is_dma")


class Prog:
    ENGS = ("pe", "act", "dve", "pool", "sp")

    def __init__(self, nc):
        self.nc = nc
        self.ops = []
        self.last_w = {}
        self.readers = {}
        self.last_on_sem = {}
        self.ps_acc = {}
        self.barrier_deps = None
        self.passed = {e: True for e in self.ENGS}

    def _add(self, eng, fn, reads, writes, sem, inc, is_dma):
        idx = len(self.ops)
        op = Op()
        op.eng = eng; op.fn = fn; op.sem = sem; op.inc = inc; op.is_dma = is_dma
        op.val = 0; op.epoch = 0; op.marked = False
        deps = set()
        ops = self.ops
        reads = [k[:2] if k[0] == "ps" else k for k in reads]
        writes = [k[:2] if k[0] == "ps" else k for k in writes]
        for k in reads:
            w = self.last_w.get(k)
            if w is not None:
                o = ops[w]
                if o.is_dma or is_dma or o.eng != eng or eng != "pe":
                    deps.add(w)
        for k in writes:
            w = self.last_w.get(k)
            if w is not None:
                o = ops[w]
                if o.is_dma or is_dma or o.eng != eng or eng != "pe":
                    deps.add(w)
            rd = self.readers.get(k)
            if rd:
                for r in rd.values():
                    o = ops[r]
                    if o.is_dma or is_dma or o.eng != eng or eng != "pe":
                        deps.add(r)
        for k in set(reads) | set(writes):
            if k[0] == "ps":
                acc = self.ps_acc.setdefault(k, {})
                for en, d in acc.items():
                    if en != eng:
                        deps.add(d)
                acc[eng] = idx
        if is_dma:
            p = self.last_on_sem.get(sem)
            if p is not None:
                deps.add(p)
        if not self.passed[eng]:
            self.passed[eng] = True
            for s, d in self.barrier_deps.items():
                o = ops[d]
                if o.is_dma or o.eng != eng:
                    deps.add(d)
        op.deps = deps
        for k in reads:
            self.readers.setdefault(k, {})[sem] = idx
        for k in writes:
            self.last_w[k] = idx
            self.readers[k] = {}
        self.last_on_sem[sem] = idx
        ops.append(op)
        return idx

    def op(self, eng, fn, reads=(), writes=()):
        return self._add(eng, fn, reads, writes, eng, 1, False)

    def dma(self, eng, slot, fn, reads=(), writes=()):
        return self._add(eng, fn, reads, writes, "dma:" + slot, 16, True)

    def barrier(self):
        self.barrier_deps = dict(self.last_on_sem)
        self.passed = {e: False for e in self.ENGS}
        self.last_w = {}
        self.readers = {}
        self.ps_acc = {}

    def emit(self):
        nc = self.nc
        ops = self.ops
        final_deps = set(self.last_on_sem.values())
        for op in ops:
            for d in op.deps:
                ops[d].marked = True
        for d in final_deps:
            ops[d].marked = True
        cnt = {}
        ep = {}
        semnames = set()
        for op in ops:
            if not op.marked:
                continue
            c = cnt.get(op.sem, 0)
            e = ep.get(op.sem, 0)
            if c + op.inc > SEM_LIMIT:
                e += 1
                c = 0
            c += op.inc
            cnt[op.sem] = c
            ep[op.sem] = e
            op.val = c
            op.epoch = e
            semnames.add((op.sem, e))
        with ExitStack() as st:
            sems = {}
            for i, key in enumerate(sorted(semnames)):
                sems[key] = st.enter_context(nc.semaphore("s%d" % i))
            block = st.enter_context(nc.Block())

            def run(engname, e):
                waited = {}
                for op in ops:
                    if op.eng != engname:
                        continue
                    for d in sorted(op.deps):
                        o = ops[d]
                        w = waited.get(o.sem)
                        if w is not None and (w[0] > o.epoch or (w[0] == o.epoch and w[1] >= o.val)):
                            continue
                        e.wait_ge(sems[(o.sem, o.epoch)], o.val)
                        waited[o.sem] = (o.epoch, o.val)
                    ins = op.fn(e)
                    if op.marked:
                        ins.then_inc(sems[(op.sem, op.epoch)], op.inc)
                if engname == "sp":
                    for d in sorted(final_deps):
                        o = ops[d]
                        w = waited.get(o.sem)
                        if w is not None and (w[0] > o.epoch or (w[0] == o.epoch and w[1] >= o.val)):
                            continue
                        e.wait_ge(sems[(o.sem, o.epoch)], o.val)
                        waited[o.sem] = (o.epoch, o.val)

            @block.tensor
            def _(e):
                run("pe", e)

            @block.scalar
            def _(e):
                run("act", e)

            @block.vector
            def _(e):
                run("dve", e)

            @block.gpsimd
            def _(e):
                run("pool", e)

            @block.sync
            def _(e):
                run("sp", e)


class SB:
    def __init__(self, big, nwords):
        self.big = big
        self.cap = nwords * 4
        self.off = 0

    def alloc(self, shape, dtype, parts=128):
        n = 1
        for s in shape:
            n *= s
        nb = n * (4 if dtype == F32 else 2)
        nb = (nb + 63) // 64 * 64
        assert self.off + nb <= self.cap, ("SBUF overflow", self.off, nb, self.cap)
        w = self.big[:, self.off // 4:(self.off + nb) // 4]
        self.off += nb
        ap = w if dtype == F32 else w.bitcast(BF16)
        ap = ap[:, 0:n]
        if len(shape) == 2:
            ap = ap.rearrange("p (a b) -> p a b", a=shape[0])
        elif len(shape) == 3:
            ap = ap.rearrange("p (a b c) -> p a b c", a=shape[0], b=shape[1])
        if parts != 128:
            ap = ap[0:parts]
        return ap

    def mark(self):
        return self.off

    def release(self, m):
        self.off = m


C_ID = 0
C_ONE = 128
C_UT = 256
C_SEL = 384
C_MUP = 512
C_MLO = 640
C_INV = 768
NCONST = 768 + 512


def make_consts():
    c = np.zeros((128, NCONST), np.float32)
    c[:, C_ID:C_ID + 128] = np.eye(128, dtype=np.float32)
    c[:, C_ONE:C_ONE + 128] = 1.0
    r = np.arange(128)[:, None]
    q = np.arange(128)[None, :]
    c[:, C_UT:C_UT + 128] = (r <= q).astype(np.float32)
    c[:, C_SEL:C_SEL + 128] = (r == 127).astype(np.float32) * np.ones((1, 128), np.float32)
    c[:, C_MUP:C_MUP + 128] = np.where(q >= r, 0.0, -30000.0)
    c[:, C_MLO:C_MLO + 128] = np.where(q < r, 0.0, 30000.0)
    for g, w in enumerate((2, 4, 8, 16)):
        inv = 1.0 / np.minimum(np.arange(1, 17), w).astype(np.float32)
        c[:, C_INV + g * 128:C_INV + (g + 1) * 128] = np.tile(inv, 8)[None, :]
    return c


class MixStage:
    def __init__(self, B):
        self.B = B
        self.bufs = [B.sb.alloc((8, TT), BF16) for _ in range(2)]
        self.n = 0
        self.cur = None

    def begin(self, G):
        slot = self.n % 2
        self.n += 1
        self.cur = (self.bufs[slot], slot, G)

    def dst(self, c):
        buf, slot, G = self.cur
        assert c // 8 == G
        return buf[:, c % 8, :], ("mxs", slot, c % 8)

    def flush(self, tt):
        buf, slot, G = self.cur
        B = self.B
        mixv = B.mixed.rearrange("(c p) s -> p c s", p=128)
        B.P.dma("sp", "mxs%d" % slot, lambda e: e.dma_start(out=mixv[:, G * 8:(G + 1) * 8, tt * TT:(tt + 1) * TT], in_=buf),
                reads=[("mxs", slot, i) for i in range(8)], writes=[("mixed", tt, G)])


class Builder:
    def __init__(self, S, layers, debug_out=None):
        self.S = S
        self.NT = S // TT
        self.layers = layers
        nc = bass.Bass("TRN2", target_bir_lowering=False)
        self.nc = nc
        self.P = Prog(nc)
        dt = nc.dram_tensor
        n_pool = max(1, sum(1 for k in layers if k[0] == "pool"))
        n_delta = max(1, sum(1 for k in layers if k[0] == "delta"))
        nl = len(layers)
        self.nl = nl
        I = "ExternalInput"
        self.x = dt("x", [S, D], F32, kind=I).ap()
        self.mem = dt("mem", [MEM, D], F32, kind=I).ap()
        self.consts = dt("consts", [128, NCONST], F32, kind=I).ap()
        self.lng = dt("lng", [128, nl * NKC], F32, kind=I).ap()
        self.memg = dt("memg", [D], F32, kind=I).ap()
        self.fing = dt("fing", [D], F32, kind=I).ap()
        self.w_in_pool = dt("w_in_pool", [n_pool, D, POOL_IN], F32, kind=I).ap()
        self.pool_maps = dt("pool_maps", [n_pool, 4, 1024, 1024], F32, kind=I).ap()
        self.pool_scale = dt("pool_scale", [128, n_pool * 32], F32, kind=I).ap()
        self.w_in_delta = dt("w_in_delta", [n_delta, D, DELTA_IN], F32, kind=I).ap()
        self.conv_w = dt("conv_w", [128, n_delta * 64 * 4], F32, kind=I).ap()
        self.a_log_rep = dt("a_log_rep", [n_delta, 128, (S // 128) * 32], F32, kind=I).ap()
        self.dt_bias_rep = dt("dt_bias_rep", [n_delta, 128, (S // 128) * 32], F32, kind=I).ap()
        self.dn_g = dt("dn_g", [n_delta, 128, 1], F32, kind=I).ap()
        self.w_kv = dt("w_kv", [nl, D, 2 * XA], F32, kind=I).ap()
        self.w_out = dt("w_out", [nl, MIX, D], F32, kind=I).ap()
        self.out = dt("out", [S, D], F32, kind="ExternalOutput").ap()
        self.hT = dt("hT", [D, S], F32).ap()
        self.proj = dt("proj", [DELTA_IN - 64, S], BF16).ap()
        self.mixed = dt("mixed", [MIX, S], BF16).ap()
        self.abt = dt("abt", [S, 64], F32).ap()

    def build(self):
        nc = self.nc
        P = self.P
        with ExitStack() as st:
            NW = 52992
            big = st.enter_context(nc.sbuf_tensor("big", [128, NW], F32))
            ps = st.enter_context(nc.psum_tensor("ps", [128, 4096], F32))
            self.sb = SB(big, NW)
            self.ps = ps
            self.bank = [ps[:, i * 512:(i + 1) * 512] for i in range(8)]
            self.setup_consts()
            self.phase_x0()
            self.phase_mem()
            for li, (kind, l, j) in enumerate(self.layers):
                self.phase_kv(li)
                if kind == "pool":
                    self.phase_inproj(li, self.w_in_pool[j], POOL_IN)
                    self.phase_pool_mix(li, j)
                else:
                    self.phase_inproj(li, self.w_in_delta[j], DELTA_IN)
                    if os.environ.get("KDBG", "") == "inproj":
                        break
                    self.phase_delta_mix(li, j)
                if os.environ.get("KDBG", ""):
                    break
                self.phase_outproj(li)
            self.phase_final()
            P.emit()
        return nc

    def setup_consts(self):
        P = self.P
        sb = self.sb
        self.cst = sb.alloc((NCONST,), F32)
        cst = self.cst
        P.dma("sp", "cst", lambda e: e.dma_start(out=cst, in_=self.consts), writes=[("cst",)])
        self.ident = cst[:, C_ID:C_ID + 128]
        self.ident_bf = sb.alloc((128,), BF16)
        self.onesD_bf = sb.alloc((128,), BF16)
        self.ones_bf = sb.alloc((128,), BF16)
        self.ones128_bf = sb.alloc((128,), BF16)
        ib, od, ob, o128 = self.ident_bf, self.onesD_bf, self.ones_bf, self.ones128_bf
        P.op("dve", lambda e: e.tensor_copy(out=ib, in_=cst[:, C_ID:C_ID + 128]), reads=[("cst",)], writes=[("cbf", 0)])
        P.op("dve", lambda e: e.tensor_scalar(out=od, in0=cst[:, C_ONE:C_ONE + 128], scalar1=1.0 / D, scalar2=None, op0=ALU.mult),
             reads=[("cst",)], writes=[("cbf", 1)])
        P.op("dve", lambda e: e.tensor_copy(out=ob, in_=cst[:, C_ONE:C_ONE + 128]), reads=[("cst",)], writes=[("cbf", 2)])
        P.op("dve", lambda e: e.tensor_scalar(out=o128, in0=cst[:, C_ONE:C_ONE + 128], scalar1=1.0 / 128, scalar2=None, op0=ALU.mult),
             reads=[("cst",)], writes=[("cbf", 3)])
        self.eps_ap = sb.alloc((1,), F32)
        ea = self.eps_ap
        P.op("dve", lambda e: e.memset(ea, EPS), writes=[("cbf", 4)])
        nl = self.nl
        self.lng_sb = sb.alloc((nl * NKC,), F32)
        lg = self.lng_sb
        P.dma("sp", "cst", lambda e: e.dma_start(out=lg, in_=self.lng), writes=[("lng",)])
        self.mem_nT = sb.alloc((NKC, MEM), BF16)
        self.KT = sb.alloc((NKC, MEM), BF16)
        self.V = sb.alloc((2, XA), BF16)
        self.persist_mark = sb.mark()

    def phase_x0(self):
        P = self.P; sb = self.sb
        m = sb.mark()
        xin = [sb.alloc((D,), F32) for _ in range(2)]
        hst = [sb.alloc((NKC, TT), F32) for _ in range(2)]
        hTv = self.hT.rearrange("(kc p) s -> p kc s", p=128)
        ident = self.ident
        n = 0
        for tt in range(self.NT):
            hs = hst[tt % 2]
            for ts in range(4):
                xb = xin[n % 2]
                tok0 = tt * TT + ts * 128
                P.dma("sp", "xin%d" % (n % 2), lambda e, xb=xb, tok0=tok0: e.dma_start(out=xb, in_=self.x[tok0:tok0 + 128, :]),
                      writes=[("xin", n % 2)])
                for q in range(4):
                    bk = (n * 4 + q) % 8
                    bank = self.bank[bk]
                    for r in range(4):
                        kc = q * 4 + r
                        P.op("pe", lambda e, bank=bank, r=r, xb=xb, kc=kc: e.transpose(out=bank[:, r * 128:(r + 1) * 128], in_=xb[:, kc * 128:(kc + 1) * 128], identity=ident),
                             reads=[("xin", n % 2), ("cst",)], writes=[("ps", bk)])
                    dst = hs[:, q * 4:(q + 1) * 4, ts * 128:(ts + 1) * 128]
                    src = bank.rearrange("p (a b) -> p a b", a=4)
                    eng = "dve" if q % 2 == 0 else "act"
                    if eng == "dve":
                        P.op("dve", lambda e, dst=dst, src=src: e.tensor_copy(out=dst, in_=src), reads=[("ps", bk)], writes=[("hst", tt % 2, ts, q)])
                    else:
                        P.op("act", lambda e, dst=dst, src=src: e.copy(out=dst, in_=src), reads=[("ps", bk)], writes=[("hst", tt % 2, ts, q)])
                n += 1
            P.dma("sp", "hst%d" % (tt % 2), lambda e, hs=hs, tt=tt: e.dma_start(out=hTv[:, :, tt * TT:(tt + 1) * TT], in_=hs),
                  reads=[("hst", tt % 2, ts, q) for ts in range(4) for q in range(4)], writes=[("hT", tt)])
        P.barrier()
        sb.release(m)

    def rms_rows(self, src, ss, junk, key_src, key_ss):
        P = self.P
        P.op("act", lambda e: e.activation(out=junk, in_=src, func=AF.Square, accum_out=ss), reads=[key_src], writes=[key_ss, ("junk",)])

    def phase_mem(self):
        P = self.P; sb = self.sb
        m = sb.mark()
        gbc = sb.alloc((D,), F32)
        P.dma("sp", "gbc", lambda e: e.dma_start(out=gbc, in_=self.memg.partition_broadcast(128)), writes=[("gbc",)])
        mt = [sb.alloc((D,), F32) for _ in range(2)]
        mn = [sb.alloc((D,), BF16) for _ in range(2)]
        junk = sb.alloc((D,), BF16)
        ss = sb.alloc((4,), F32)
        for mc in range(2):
            t = mt[mc]
            P.dma("sp", "mt%d" % mc, lambda e, t=t, mc=mc: e.dma_start(out=t, in_=self.mem[mc * 128:(mc + 1) * 128, :]), writes=[("mt", mc)])
            s1 = ss[:, mc * 2:mc * 2 + 1]
            s2 = ss[:, mc * 2 + 1:mc * 2 + 2]
            P.op("act", lambda e, t=t, s1=s1: e.activation(out=junk, in_=t, func=AF.Square, accum_out=s1), reads=[("mt", mc)], writes=[("ss", mc), ("junk",)])
            P.op("act", lambda e, s1=s1, s2=s2: e.activation(out=s2, in_=s1, func=AF.Sqrt, bias=self.eps_ap, scale=1.0 / D),
                 reads=[("ss", mc), ("cbf", 4)], writes=[("ss2", mc)])
            P.op("dve", lambda e, s2=s2: e.reciprocal(out=s2, in_=s2), reads=[("ss2", mc)], writes=[("ss2", mc)])
            o = mn[mc]
            P.op("dve", lambda e, o=o, t=t, s2=s2: e.scalar_tensor_tensor(out=o, in0=t, scalar=s2, in1=gbc, op0=ALU.mult, op1=ALU.mult),
                 reads=[("mt", mc), ("ss2", mc), ("gbc",)], writes=[("mn", mc)])
            for q in range(4):
                bk = (mc * 4 + q) % 8
                bank = self.bank[bk].bitcast(BF16)
                for r in range(4):
                    kc = q * 4 + r
                    P.op("pe", lambda e, bank=bank, r=r, o=o, kc=kc: e.transpose(out=bank[:, r * 128:(r + 1) * 128], in_=o[:, kc * 128:(kc + 1) * 128], identity=self.ident_bf),
                         reads=[("mn", mc), ("cbf", 0)], writes=[("ps", bk)])
                dst = self.mem_nT[:, q * 4:(q + 1) * 4, mc * 128:(mc + 1) * 128]
                src = bank[:, 0:512].rearrange("p (a b) -> p a b", a=4)
                P.op("dve", lambda e, dst=dst, src=src: e.tensor_copy(out=dst, in_=src), reads=[("ps", bk)], writes=[("memnT",)])
        P.barrier()
        sb.release(m)

    def phase_kv(self, li):
        P = self.P; sb = self.sb
        m = sb.mark()
        wb = [sb.alloc((NKC, 512), BF16) for _ in range(2)]
        wv = self.w_kv[li].rearrange("(kc p) n -> p kc n", p=128)
        n = 0
        for blk in range(8):
            w = wb[blk % 2]
            P.dma("pool", "wkv%d" % (blk % 2), lambda e, w=w, blk=blk: e.dma_start(out=w, in_=wv[:, :, blk * 512:(blk + 1) * 512]),
                  writes=[("wkv", blk % 2)])
            if blk < 4:
                for sub in range(4):
                    bk = n % 8; n += 1
                    bank = self.bank[bk]
                    for kc in range(NKC):
                        P.op("pe", lambda e, bank=bank, w=w, kc=kc, sub=sub: e.matmul(bank[:, 0:MEM], w[:, kc, sub * 128:(sub + 1) * 128], self.mem_nT[:, kc, :], start=(kc == 0), stop=(kc == NKC - 1)),
                             reads=[("wkv", blk % 2), ("memnT",)], writes=[("ps", bk)])
                    dst = self.KT[:, blk * 4 + sub, :]
                    P.op("act", lambda e, dst=dst, bank=bank: e.copy(out=dst, in_=bank[:, 0:MEM]), reads=[("ps", bk)], writes=[("KT",)])
            else:
                for mc in range(2):
                    bk = n % 8; n += 1
                    bank = self.bank[bk]
                    for kc in range(NKC):
                        P.op("pe", lambda e, bank=bank, w=w, kc=kc, mc=mc: e.matmul(bank, self.mem_nT[:, kc, mc * 128:(mc + 1) * 128], w[:, kc, :], start=(kc == 0), stop=(kc == NKC - 1)),
                             reads=[("wkv", blk % 2), ("memnT",)], writes=[("ps", bk)])
                    dst = self.V[:, mc, (blk - 4) * 512:(blk - 3) * 512]
                    P.op("dve", lambda e, dst=dst, bank=bank: e.tensor_copy(out=dst, in_=bank), reads=[("ps", bk)], writes=[("V",)])
        P.barrier()
        sb.release(m)

    def phase_inproj(self, li, W, ncols):
        P = self.P; sb = self.sb; S = self.S; NT = self.NT
        m = sb.mark()
        xnT = sb.alloc((NKC, S), BF16)
        rstd = sb.alloc((S,), F32)
        self.rstd_all = rstd
        m2 = sb.mark()
        hb = [sb.alloc((4, TT), F32) for _ in range(2)]
        sq = [sb.alloc((TT,), BF16) for _ in range(2)]
        hTv = self.hT.rearrange("(kc p) s -> p kc s", p=128)
        lg = self.lng_sb
        n = 0
        for tt in range(NT):
            bk = tt % 2
            bank = self.bank[bk]
            for q in range(4):
                h = hb[n % 2]
                P.dma("sp", "hb%d" % (n % 2), lambda e, h=h, q=q, tt=tt: e.dma_start(out=h, in_=hTv[:, q * 4:(q + 1) * 4, tt * TT:(tt + 1) * TT]),
                      reads=[("hT", tt)], writes=[("hb", n % 2)])
                for r in range(4):
                    kc = q * 4 + r
                    s = sq[kc % 2]
                    P.op("act", lambda e, s=s, h=h, r=r: e.activation(out=s, in_=h[:, r, :], func=AF.Square), reads=[("hb", n % 2)], writes=[("sq", kc % 2)])
                    P.op("pe", lambda e, bank=bank, s=s, kc=kc: e.matmul(bank, self.onesD_bf, s, start=(kc == 0), stop=(kc == NKC - 1)),
                         reads=[("sq", kc % 2), ("cbf", 1)], writes=[("ps", bk)])
                    dst = xnT[:, kc, tt * TT:(tt + 1) * TT]
                    g = lg[:, li * NKC + kc:li * NKC + kc + 1]
                    P.op("dve", lambda e, dst=dst, h=h, r=r, g=g: e.tensor_scalar(out=dst, in0=h[:, r, :], scalar1=g, scalar2=None, op0=ALU.mult),
                         reads=[("hb", n % 2), ("lng",)], writes=[("xnT", tt)])
                n += 1
            rs = rstd[:, tt * TT:(tt + 1) * TT]
            P.op("act", lambda e, rs=rs, bank=bank: e.activation(out=rs, in_=bank, func=AF.Sqrt, bias=self.eps_ap, scale=1.0),
                 reads=[("ps", bk), ("cbf", 4)], writes=[("rstd", tt)])
            P.op("dve", lambda e, rs=rs: e.reciprocal(out=rs, in_=rs), reads=[("rstd", tt)], writes=[("rstd", tt)])
        P.barrier()
        sb.release(m2)
        CB = 256
        Wv = W.rearrange("(kc p) n -> p kc n", p=128)
        nfull = (ncols // CB)
        m3 = sb.mark()
        if ncols % CB:
            rem = ncols - nfull * CB
            wt = sb.alloc((NKC, rem), BF16)
            P.dma("pool", "wtail", lambda e: e.dma_start(out=wt, in_=Wv[:, :, nfull * CB:ncols]), writes=[("wtail",)])
            abs_ = [sb.alloc((rem,), F32) for _ in range(2)]
            rt = [sb.alloc((1,), F32) for _ in range(2)]
            for t in range(S // 128):
                bk = 4 + (t % 2) * 2
                bank = self.bank[bk]
                bank2 = self.bank[bk + 1]
                for kc in range(NKC):
                    P.op("pe", lambda e, bank=bank, kc=kc, t=t: e.matmul(bank[:, 0:rem], xnT[:, kc, t * 128:(t + 1) * 128], wt[:, kc, :], start=(kc == 0), stop=(kc == NKC - 1)),
                         reads=[("wtail",), ("xnT", t // 4)], writes=[("ps", bk)])
                P.op("pe", lambda e, bank2=bank2, t=t: e.transpose(out=bank2[:, 0:128], in_=rstd[:, t * 128:(t + 1) * 128], identity=self.ident),
                     reads=[("rstd", t // 4), ("cst",)], writes=[("ps", bk + 1)])
                r1 = rt[t % 2]
                P.op("act", lambda e, r1=r1, bank2=bank2: e.copy(out=r1, in_=bank2[:, 0:1]), reads=[("ps", bk + 1)], writes=[("rt", t % 2)])
                a = abs_[t % 2]
                P.op("dve", lambda e, a=a, bank=bank, r1=r1: e.tensor_scalar(out=a, in0=bank[:, 0:rem], scalar1=r1, scalar2=None, op0=ALU.mult),
                     reads=[("ps", bk), ("rt", t % 2)], writes=[("abs", t % 2)])
                P.dma("sp", "abs%d" % (t % 2), lambda e, a=a, t=t: e.dma_start(out=self.abt[t * 128:(t + 1) * 128, :], in_=a),
                      reads=[("abs", t % 2)], writes=[("abt", t)])
            P.barrier()
        sb.release(m3)
        wb = [sb.alloc((NKC, CB), BF16) for _ in range(2)]
        ost = [sb.alloc((S,), BF16) for _ in range(2)]
        n = 0
        no = 0
        for cb in range(nfull):
            w = wb[cb % 2]
            P.dma("pool", "win%d" % (cb % 2), lambda e, w=w, cb=cb: e.dma_start(out=w, in_=Wv[:, :, cb * CB:(cb + 1) * CB]), writes=[("win", cb % 2)])
            for sub in range(CB // 128):
                o = ost[no % 2]
                for tt in range(NT):
                    bk = n % 4; n += 1
                    bank = self.bank[bk]
                    for kc in range(NKC):
                        P.op("pe", lambda e, bank=bank, w=w, kc=kc, sub=sub, tt=tt: e.matmul(bank, w[:, kc, sub * 128:(sub + 1) * 128], xnT[:, kc, tt * TT:(tt + 1) * TT], start=(kc == 0), stop=(kc == NKC - 1)),
                             reads=[("win", cb % 2), ("xnT", tt)], writes=[("ps", bk)])
                    dst = o[:, tt * TT:(tt + 1) * TT]
                    rs = rstd[:, tt * TT:(tt + 1) * TT]
                    P.op("dve", lambda e, dst=dst, bank=bank, rs=rs: e.tensor_tensor(out=dst, in0=bank, in1=rs, op=ALU.mult),
                         reads=[("ps", bk), ("rstd", tt)], writes=[("ost", no % 2, tt)])
                row0 = cb * CB + sub * 128
                P.dma("sp", "ost%d" % (no % 2), lambda e, o=o, row0=row0: e.dma_start(out=self.proj[row0:row0 + 128, :], in_=o),
                      reads=[("ost", no % 2, tt) for tt in range(NT)], writes=[("proj", row0 // 128)])
                no += 1
        P.barrier()
        sb.release(m)

    def attention_tile(self, li, tt, qrow0, sz_loader, mix):
        P = self.P; sb = self.sb
        scale = 512.0 ** -0.5
        projv = self.proj.rearrange("(c p) s -> p c s", p=128)
        c0 = qrow0 // 128
        n = self.att_n
        for h in range(4):
            if h % 2 == 0:
                mix.begin(4 + h // 2)
            qs = self.att_nq % 2; self.att_nq += 1
            qT = self.att_q[qs]
            P.dma("sp", "attq%d" % qs, lambda e, qT=qT, h=h: e.dma_start(out=qT, in_=projv[:, c0 + h * 4:c0 + h * 4 + 4, tt * TT:(tt + 1) * TT]),
                  reads=[("proj", c0 + h * 4 + i) for i in range(4)], writes=[("attq", qs)])
            PT = self.att_PT[h % 2]
            for ts in range(4):
                bk = 4 + n % 2
                bank = self.bank[bk]
                for dc in range(4):
                    P.op("pe", lambda e, bank=bank, h=h, dc=dc, ts=ts, qT=qT: e.matmul(bank[:, 0:MEM], qT[:, dc, ts * 128:(ts + 1) * 128], self.KT[:, h * 4 + dc, :], start=(dc == 0), stop=(dc == 3)),
                         reads=[("attq", qs), ("KT",)], writes=[("ps", bk)])
                i = n % 2
                mx = self.att_small[:, i * 4 + 0:i * 4 + 1]
                sm = self.att_small[:, i * 4 + 1:i * 4 + 2]
                rs = self.att_small[:, i * 4 + 2:i * 4 + 3]
                pe_ = self.att_p[i]
                pn = self.att_pn[i]
                P.op("dve", lambda e, mx=mx, bank=bank: e.tensor_reduce(out=mx, in_=bank[:, 0:MEM], axis=AX.X, op=ALU.max), reads=[("ps", bk)], writes=[("amx", i)])
                P.op("dve", lambda e, mx=mx: e.tensor_scalar(out=mx, in0=mx, scalar1=-scale, scalar2=None, op0=ALU.mult), reads=[("amx", i)], writes=[("amx", i)])
                P.op("act", lambda e, pe_=pe_, bank=bank, mx=mx, sm=sm: e.activation(out=pe_, in_=bank[:, 0:MEM], func=AF.Exp, bias=mx, scale=scale, accum_out=sm),
                     reads=[("ps", bk), ("amx", i)], writes=[("ap", i), ("asm", i)])
                P.op("dve", lambda e, rs=rs, sm=sm: e.reciprocal(out=rs, in_=sm), reads=[("asm", i)], writes=[("ars", i)])
                P.op("dve", lambda e, pn=pn, pe_=pe_, rs=rs: e.tensor_scalar(out=pn, in0=pe_, scalar1=rs, scalar2=None, op0=ALU.mult),
                     reads=[("ap", i), ("ars", i)], writes=[("apn", i)])
                bk2 = 6 + n % 2
                bank2 = self.bank[bk2].bitcast(BF16)
                for mc in range(2):
                    P.op("pe", lambda e, bank2=bank2, pn=pn, mc=mc: e.transpose(out=bank2[:, mc * 128:(mc + 1) * 128], in_=pn[:, mc * 128:(mc + 1) * 128], identity=self.ident_bf),
                         reads=[("apn", i), ("cbf", 0)], writes=[("ps", bk2)])
                dst = PT[:, :, ts * 128:(ts + 1) * 128]
                src = bank2[:, 0:256].rearrange("p (a b) -> p a b", a=2)
                P.op("act", lambda e, dst=dst, src=src: e.copy(out=dst, in_=src), reads=[("ps", bk2)], writes=[("aPT", h % 2, ts)])
                n += 1
            for dc in range(4):
                bk = (h * 4 + dc) % 4
                bank = self.bank[bk]
                for mc in range(2):
                    P.op("pe", lambda e, bank=bank, mc=mc, h=h, dc=dc, PT=PT: e.matmul(bank, self.V[:, mc, (h * 4 + dc) * 128:(h * 4 + dc + 1) * 128], PT[:, mc, :], start=(mc == 0), stop=(mc == 1)),
                         reads=[("V",)] + [("aPT", h % 2, ts) for ts in range(4)], writes=[("ps", bk)])
                c = 32 + h * 4 + dc
                szc, szkey = sz_loader(c)
                dst, dkey = mix.dst(c)
                P.op("dve", lambda e, dst=dst, bank=bank, szc=szc: e.tensor_tensor(out=dst, in0=bank, in1=szc, op=ALU.mult),
                     reads=[("ps", bk), szkey], writes=[dkey])
            if h % 2 == 1:
                mix.flush(tt)
        self.att_n = n

    def alloc_attention(self):
        sb = self.sb
        self.att_q = [sb.alloc((4, TT), BF16) for _ in range(2)]
        self.att_PT = [sb.alloc((2, TT), BF16) for _ in range(2)]
        self.att_p = [sb.alloc((MEM,), F32) for _ in range(2)]
        self.att_pn = [sb.alloc((MEM,), BF16) for _ in range(2)]
        self.att_small = sb.alloc((8,), F32)
        self.att_n = 0
        self.att_nq = 0

    def phase_pool_mix(self, li, j):
        P = self.P; sb = self.sb; NT = self.NT
        m = sb.mark()
        projv = self.proj.rearrange("(c p) s -> p c s", p=128)
        psc = sb.alloc((32,), F32)
        P.dma("sp", "psc", lambda e: e.dma_start(out=psc, in_=self.pool_scale[:, j * 32:(j + 1) * 32]), writes=[("psc",)])
        mix = MixStage(self)
        ub = [sb.alloc((8, TT + 16), BF16) for _ in range(2)]
        db = [sb.alloc((8, TT), BF16) for _ in range(2)]
        t1 = sb.alloc((4, TT + 16), F32)
        t2 = sb.alloc((4, TT + 16), F32)
        mp = [sb.alloc((8, 1024), BF16) for _ in range(2)]
        zb = [sb.alloc((8, TT), BF16) for _ in range(2)]
        self.alloc_attention()
        cst = self.cst
        zc0 = BR // 128
        ng = 0
        nz = 0
        nb = 0
        for tt in range(NT):
            zstate = {}

            def load_z(grp, tt=tt):
                nonlocal nz
                zt = zb[nz % 2]
                slot = nz % 2
                nz += 1
                P.dma("sp", "zb%d" % slot, lambda e, zt=zt, grp=grp: e.dma_start(out=zt, in_=projv[:, zc0 + grp * 8:zc0 + grp * 8 + 8, tt * TT:(tt + 1) * TT]),
                      reads=[("proj", zc0 + grp * 8 + i) for i in range(8)], writes=[("zb", slot)])
                P.op("act", lambda e, zt=zt: e.activation(out=zt, in_=zt, func=AF.Silu), reads=[("zb", slot)], writes=[("zb", slot)])
                zstate[grp] = (zt, slot)

            def sz_loader(c):
                grp = c // 8
                if grp not in zstate:
                    load_z(grp)
                zt, slot = zstate[grp]
                return zt[:, c % 8, :], ("zb", slot)

            for g in range(4):
                w = 2 ** (g + 1)
                u = ub[ng % 2]; d = db[ng % 2]; us = ng % 2
                mw = mp[ng % 2]
                ng += 1
                mix.begin(g)
                P.dma("pool", "mp%d" % us, lambda e, mw=mw, g=g: e.dma_start(out=mw, in_=self.pool_maps[j, g].rearrange("(cc p) n -> p cc n", p=128)),
                      writes=[("mp", us)])
                if tt == 0:
                    P.op("pool", lambda e, u=u: e.memset(u[:, :, 0:16], 0.0), writes=[("ub", us)])
                    P.dma("sp", "ub%d" % us, lambda e, u=u, g=g: e.dma_start(out=u[:, :, 16:16 + TT], in_=projv[:, g * 8:g * 8 + 8, 0:TT]),
                          reads=[("proj", g * 8 + i) for i in range(8)], writes=[("ub", us)])
                else:
                    P.dma("sp", "ub%d" % us, lambda e, u=u, g=g, tt=tt: e.dma_start(out=u, in_=projv[:, g * 8:g * 8 + 8, tt * TT - 16:(tt + 1) * TT]),
                          reads=[("proj", g * 8 + i) for i in range(8)], writes=[("ub", us)])
                L = TT + 16
                for hf in range(2):
                    uh = u[:, hf * 4:(hf + 1) * 4, :]
                    dh = d[:, hf * 4:(hf + 1) * 4, :]
                    cur = uh
                    sh = 1
                    k = 0
                    while sh < w:
                        dstt = t1 if k % 2 == 0 else t2
                        P.op("dve", lambda e, dstt=dstt, cur=cur, sh=sh: e.tensor_tensor(out=dstt[:, :, 2 * sh - 1:L], in0=cur[:, :, 2 * sh - 1:L], in1=cur[:, :, sh - 1:L - sh], op=ALU.add),
                             reads=[("ub", us), ("t1",), ("t2",)], writes=[("t1",) if k % 2 == 0 else ("t2",)])
                        cur = dstt
                        sh *= 2
                        k += 1
                    P.op("dve", lambda e, dh=dh, cur=cur, uh=uh, w=w: e.scalar_tensor_tensor(out=dh, in0=cur[:, :, 16:L], scalar=1.0 / w, in1=uh[:, :, 16:L], op0=ALU.mult, op1=ALU.subtract),
                         reads=[("ub", us), ("t1",), ("t2",)], writes=[("db", us)])
                    if tt == 0:
                        inv = cst[:, C_INV + g * 128:C_INV + g * 128 + 64].rearrange("p (a b) -> p a b", a=4)
                        other = t2 if cur is t1 else t1
                        P.op("dve", lambda e, other=other, cur=cur, inv=inv: e.tensor_tensor(out=other[:, :, 0:16], in0=cur[:, :, 16:32], in1=inv, op=ALU.mult),
                             reads=[("t1",), ("t2",), ("cst",)], writes=[("t1",), ("t2",)])
                        P.op("dve", lambda e, dh=dh, other=other, uh=uh: e.tensor_tensor(out=dh[:, :, 0:16], in0=other[:, :, 0:16], in1=uh[:, :, 16:32], op=ALU.subtract),
                             reads=[("t1",), ("t2",), ("ub", us)], writes=[("db", us)])
                for oc in range(8):
                    bk = nb % 4; nb += 1
                    bank = self.bank[bk]
                    for cc in range(8):
                        P.op("pe", lambda e, bank=bank, mw=mw, cc=cc, oc=oc, d=d: e.matmul(bank, mw[:, cc, oc * 128:(oc + 1) * 128], d[:, cc, :], start=(cc == 0), stop=(cc == 7)),
                             reads=[("mp", us), ("db", us)], writes=[("ps", bk)])
                    c = g * 8 + oc
                    szc, szkey = sz_loader(c)
                    dst, dkey = mix.dst(c)
                    sc = psc[:, c:c + 1]
                    P.op("dve", lambda e, dst=dst, bank=bank, sc=sc, szc=szc: e.scalar_tensor_tensor(out=dst, in0=bank, scalar=sc, in1=szc, op0=ALU.mult, op1=ALU.mult),
                         reads=[("ps", bk), ("psc",), szkey], writes=[dkey])
                mix.flush(tt)
            self.attention_tile(li, tt, BR + MIX, sz_loader, mix)
        P.barrier()
        sb.release(m)

    def phase_delta_mix(self, li, j):
        P = self.P; sb = self.sb; S = self.S; NT = self.NT
        NC = S // 128
        m = sb.mark()
        projv = self.proj.rearrange("(c p) s -> p c s", p=128)
        cst = self.cst
        ONES = cst[:, C_ONE:C_ONE + 128]
        IDN = cst[:, C_ID:C_ID + 128]
        MUP = cst[:, C_MUP:C_MUP + 128]
        MLO = cst[:, C_MLO:C_MLO + 128]
        one_col = cst[:, C_ONE:C_ONE + 1]
        G = NC * 32
        gc = sb.alloc((G,), F32); ngc = sb.alloc((G,), F32); beta = sb.alloc((G,), F32); nbeta = sb.alloc((G,), F32)
        bg = sb.alloc((G,), F32); kd = sb.alloc((G,), F32); egl = sb.alloc((G,), F32)
        cw = sb.alloc((256,), F32)
        dng = sb.alloc((1,), F32)
        P.dma("sp", "cw", lambda e: e.dma_start(out=cw, in_=self.conv_w[:, j * 256:(j + 1) * 256]), writes=[("cw",)])
        P.dma("sp", "dng", lambda e: e.dma_start(out=dng, in_=self.dn_g[j]), writes=[("dng",)])
        mg = sb.mark()
        AB = sb.alloc((NC, 64), F32)
        alr = sb.alloc((G,), F32); dtr = sb.alloc((G,), F32)
        x1 = sb.alloc((G,), F32); x2 = sb.alloc((G,), F32); x3 = sb.alloc((G,), F32)
        P.dma("sp", "AB", lambda e: e.dma_start(out=AB, in_=self.abt.rearrange("(c p) f -> p c f", p=128)), writes=[("AB",)])
        P.dma("sp", "alr", lambda e: e.dma_start(out=alr, in_=self.a_log_rep[j]), writes=[("alr",)])
        P.dma("sp", "dtr", lambda e: e.dma_start(out=dtr, in_=self.dt_bias_rep[j]), writes=[("dtr",)])
        v3 = lambda t: t.rearrange("p (c h) -> p c h", c=NC)
        bl = AB[:, :, 0:32]; al = AB[:, :, 32:64]
        P.op("act", lambda e: e.activation(out=v3(beta), in_=bl, func=AF.Sigmoid), reads=[("AB",)], writes=[("beta",)])
        P.op("dve", lambda e: e.tensor_scalar(out=nbeta, in0=beta, scalar1=-1.0, scalar2=None, op0=ALU.mult), reads=[("beta",)], writes=[("nbeta",)])
        P.op("dve", lambda e: e.tensor_tensor(out=v3(x1), in0=al, in1=v3(dtr), op=ALU.add), reads=[("AB",), ("dtr",)], writes=[("x1",)])
        P.op("act", lambda e: e.activation(out=x2, in_=x1, func=AF.Abs), reads=[("x1",)], writes=[("x2",)])
        P.op("act", lambda e: e.activation(out=x2, in_=x2, func=AF.Exp, scale=-1.0), reads=[("x2",)], writes=[("x2",)])
        P.op("act", lambda e: e.activation(out=x2, in_=x2, func=AF.Ln, bias=one_col, scale=1.0), reads=[("x2",), ("cst",)], writes=[("x2",)])
        P.op("dve", lambda e: e.tensor_scalar(out=x1, in0=x1, scalar1=0.0, scalar2=None, op0=ALU.max), reads=[("x1",)], writes=[("x1",)])
        P.op("dve", lambda e: e.tensor_tensor(out=x1, in0=x1, in1=x2, op=ALU.add), reads=[("x1",), ("x2",)], writes=[("x1",)])
        P.op("act", lambda e: e.activation(out=x3, in_=alr, func=AF.Exp), reads=[("alr",)], writes=[("x3",)])
        P.op("dve", lambda e: e.scalar_tensor_tensor(out=x1, in0=x1, scalar=-1.0, in1=x3, op0=ALU.mult, op1=ALU.mult), reads=[("x1",), ("x3",)], writes=[("x1",)])
        GS = min(512, G)
        for hf in range(G // GS):
            sl = slice(hf * GS, (hf + 1) * GS)
            bk = hf % 2
            P.op("pe", lambda e, sl=sl, bk=bk: e.matmul(self.bank[bk][:, 0:GS], cst[:, C_UT:C_UT + 128], x1[:, sl], start=True, stop=True), reads=[("x1",), ("cst",)], writes=[("ps", bk)])
            P.op("act", lambda e, sl=sl, bk=bk: e.copy(out=gc[:, sl], in_=self.bank[bk][:, 0:GS]), reads=[("ps", bk)], writes=[("gc",)])
            P.op("pe", lambda e, sl=sl, bk=bk: e.matmul(self.bank[bk + 2][:, 0:GS], ONES, x1[:, sl], start=True, stop=True), reads=[("x1",), ("cst",)], writes=[("ps", bk + 2)])
            P.op("act", lambda e, sl=sl, bk=bk: e.copy(out=x2[:, sl], in_=self.bank[bk + 2][:, 0:GS]), reads=[("ps", bk + 2)], writes=[("x2",)])
        P.op("dve", lambda e: e.tensor_scalar(out=ngc, in0=gc, scalar1=-1.0, scalar2=None, op0=ALU.mult), reads=[("gc",)], writes=[("ngc",)])
        P.op("act", lambda e: e.activation(out=egl, in_=x2, func=AF.Exp), reads=[("x2",)], writes=[("egl",)])
        P.op("dve", lambda e: e.tensor_tensor(out=x2, in0=x2, in1=gc, op=ALU.subtract), reads=[("x2",), ("gc",)], writes=[("x2",)])
        P.op("act", lambda e: e.activation(out=kd, in_=x2, func=AF.Exp), reads=[("x2",)], writes=[("kd",)])
        P.op("act", lambda e: e.activation(out=x3, in_=gc, func=AF.Exp), reads=[("gc",)], writes=[("x3",)])
        P.op("dve", lambda e: e.tensor_tensor(out=bg, in0=x3, in1=nbeta, op=ALU.mult), reads=[("x3",), ("nbeta",)], writes=[("bg",)])
        P.barrier()
        sb.release(mg)
        if os.environ.get("KDBG", "") == "gates":
            sb.release(m)
            return
        raw = [sb.alloc((S + 3,), BF16)] * 2
        acc = sb.alloc((S,), F32)
        sqb = [sb.alloc((TT,), BF16) for _ in range(2)]
        rnb = [sb.alloc((TT,), F32) for _ in range(2)]
        qT = sb.alloc((S,), BF16); kT = sb.alloc((S,), BF16)
        vT = [sb.alloc((S,), BF16) for _ in range(2)]
        oT = [sb.alloc((S,), F32) for _ in range(2)]
        ost = [sb.alloc((S,), BF16) for _ in range(2)]
        zb = [sb.alloc((TT,), BF16) for _ in range(2)]
        S32 = [sb.alloc((128,), F32) for _ in range(2)]
        Sbf = [sb.alloc((128,), BF16) for _ in range(2)]
        def pe2(shape, dt_):
            return [[sb.alloc(shape, dt_) for _ in range(2)] for _ in range(2)]
        KQ = [sb.alloc((256,), F32) for _ in range(2)]
        ktok = [sb.alloc((128,), BF16) for _ in range(2)]
        Vb = pe2((128,), F32); Kd = pe2((128,), BF16); Dbf = pe2((128,), F32)
        Slo = [sb.alloc((128,), BF16) for _ in range(2)]
        nbg = bg
        Rd = pe2((128,), F32); E2 = pe2((128,), F32); E2T = pe2((128,), F32); EG = pe2((128,), F32)
        Mt = pe2((128,), F32); MP = pe2((256,), F32); TTt = pe2((128,), F32)
        QKd = pe2((128,), BF16); QgT = pe2((128,), BF16)
        vnew = pe2((128,), BF16)
        bank = self.bank
        nraw = 0
        nsq = 0
        nz = 0
        qscale = 128.0 ** -0.5
        for g in range(16):
            for which in range(4):
                ci = (g, 16 + g, 32 + 2 * g, 33 + 2 * g)[which]
                r = raw[0]; rs_ = 0; nraw += 1
                P.op("pool", lambda e, r=r: e.memset(r[:, 0:3], 0.0), writes=[("raw", rs_)])
                P.dma("sp", "raw%d" % rs_, lambda e, r=r, ci=ci: e.dma_start(out=r[:, 3:3 + S], in_=projv[:, ci, :]), reads=[("proj", ci)], writes=[("raw", rs_)])
                w = [cw[:, ci * 4 + t:ci * 4 + t + 1] for t in range(4)]
                P.op("dve", lambda e, r=r, w=w: e.tensor_scalar(out=acc, in0=r[:, 0:S], scalar1=w[0], scalar2=None, op0=ALU.mult), reads=[("raw", rs_), ("cw",)], writes=[("acc",)])
                for t in range(1, 4):
                    P.op("dve", lambda e, r=r, w=w, t=t: e.scalar_tensor_tensor(out=acc, in0=r[:, t:t + S], scalar=w[t], in1=acc, op0=ALU.mult, op1=ALU.add),
                         reads=[("raw", rs_), ("cw",), ("acc",)], writes=[("acc",)])
                if which >= 2:
                    dstv = vT[which - 2]
                    P.op("act", lambda e, dstv=dstv: e.activation(out=dstv, in_=acc, func=AF.Silu), reads=[("acc",)], writes=[("vT", which - 2)])
                else:
                    dstq = qT if which == 0 else kT
                    dk_ = ("qT",) if which == 0 else ("kT",)
                    P.op("act", lambda e: e.activation(out=acc, in_=acc, func=AF.Silu), reads=[("acc",)], writes=[("acc",)])
                    for tt in range(NT):
                        i = nsq % 2; nsq += 1
                        sl = slice(tt * TT, (tt + 1) * TT)
                        bk = 6 + i
                        P.op("act", lambda e, i=i, sl=sl: e.activation(out=sqb[i], in_=acc[:, sl], func=AF.Square), reads=[("acc",)], writes=[("sqb", i)])
                        P.op("pe", lambda e, i=i, bk=bk: e.matmul(bank[bk], self.ones_bf, sqb[i], start=True, stop=True), reads=[("sqb", i), ("cbf", 2)], writes=[("ps", bk, 0), ("ps", bk, 1), ("ps", bk, 2), ("ps", bk, 3)])
                        P.op("act", lambda e, i=i, bk=bk: e.activation(out=rnb[i], in_=bank[bk], func=AF.Sqrt, bias=self.eps_ap, scale=1.0),
                             reads=[("ps", bk, 0), ("ps", bk, 1), ("ps", bk, 2), ("ps", bk, 3), ("cbf", 4)], writes=[("rnb", i)])
                        P.op("dve", lambda e, i=i: e.reciprocal(out=rnb[i], in_=rnb[i]), reads=[("rnb", i)], writes=[("rnb", i)])
                        if which == 0:
                            P.op("dve", lambda e, i=i, sl=sl, dstq=dstq: e.scalar_tensor_tensor(out=dstq[:, sl], in0=acc[:, sl], scalar=qscale, in1=rnb[i], op0=ALU.mult, op1=ALU.mult),
                                 reads=[("acc",), ("rnb", i)], writes=[dk_])
                        else:
                            P.op("dve", lambda e, i=i, sl=sl, dstq=dstq: e.tensor_tensor(out=dstq[:, sl], in0=acc[:, sl], in1=rnb[i], op=ALU.mult),
                                 reads=[("acc",), ("rnb", i)], writes=[dk_])
            if os.environ.get("KDBG", "") == "prep":
                P.barrier(); sb.release(m)
                return
            for e_ in range(2):
                P.op("pool", lambda e, e_=e_: e.memset(S32[e_], 0.0), writes=[("S32", e_)])
                P.op("pool", lambda e, e_=e_: e.memset(Sbf[e_], 0.0), writes=[("Sbf", e_)])
                P.op("pool", lambda e, e_=e_: e.memset(Slo[e_], 0.0), writes=[("Slo", e_)])
            for c in range(NC):
                p = c % 2
                cs = slice(c * 128, (c + 1) * 128)
                A0 = p * 256
                bA = bank[0]; bB = bank[1].bitcast(BF16)
                B0 = p * 512
                kA = ("ps", 0); kB = ("ps", 1)
                P.op("pe", lambda e, cs=cs, A0=A0: e.matmul(bA[:, A0:A0 + 128], kT[:, cs], kT[:, cs], start=True, stop=True), reads=[("kT",)], writes=[kA])
                P.op("pe", lambda e, cs=cs, A0=A0: e.matmul(bA[:, A0 + 128:A0 + 256], kT[:, cs], qT[:, cs], start=True, stop=True), reads=[("kT",), ("qT",)], writes=[kA])
                P.op("pe", lambda e, cs=cs, B0=B0: e.transpose(out=bB[:, B0:B0 + 128], in_=kT[:, cs], identity=self.ident_bf), reads=[("kT",), ("cbf", 0)], writes=[kB])
                for e_ in range(2):
                    P.op("pe", lambda e, cs=cs, B0=B0, e_=e_: e.transpose(out=bB[:, B0 + 128 * (1 + e_):B0 + 128 * (2 + e_)], in_=vT[e_][:, cs], identity=self.ident_bf),
                         reads=[("vT", e_), ("cbf", 0)], writes=[kB])
                P.op("act", lambda e, p=p, A0=A0: e.copy(out=KQ[p], in_=bA[:, A0:A0 + 256]), reads=[kA], writes=[("KQ", p)])
                P.op("dve", lambda e, p=p, B0=B0: e.tensor_copy(out=ktok[p], in_=bB[:, B0:B0 + 128]), reads=[kB], writes=[("ktok", p)])
                col = [c * 32 + 2 * g + e_ for e_ in range(2)]
                for e_ in range(2):
                    cl = col[e_]
                    vsrc = bB[:, B0 + 128 * (1 + e_):B0 + 128 * (2 + e_)]
                    P.op("dve", lambda e, p=p, e_=e_, cl=cl, vsrc=vsrc: e.tensor_scalar(out=Vb[p][e_], in0=vsrc, scalar1=beta[:, cl:cl + 1], scalar2=None, op0=ALU.mult),
                         reads=[kB, ("beta",)], writes=[("Vb", p, e_)])
                    P.op(POOLENG, lambda e, p=p, e_=e_, cl=cl: e.tensor_scalar(out=Kd[p][e_], in0=ktok[p], scalar1=kd[:, cl:cl + 1], scalar2=None, op0=ALU.mult),
                         reads=[("ktok", p), ("kd",)], writes=[("Kd", p, e_)])
                    P.op("dve", lambda e, p=p, e_=e_, cl=cl: e.tensor_scalar(out=Rd[p][e_], in0=IDN, scalar1=gc[:, cl:cl + 1], scalar2=None, op0=ALU.mult),
                         reads=[("cst",), ("gc",)], writes=[("Rd", p, e_)])
                    bC = bank[2 + e_]
                    kC = ("ps", 2 + e_)
                    P.op("pe", lambda e, p=p, e_=e_, bC=bC: e.matmul(bC[:, 0:128], ONES, Rd[p][e_], start=True, stop=True), reads=[("cst",), ("Rd", p, e_)], writes=[kC])
                    P.op("pe", lambda e, p=p, e_=e_, bC=bC: e.matmul(bC[:, 128:256], ONES, Rd[p][e_], start=True, stop=False), reads=[("cst",), ("Rd", p, e_)], writes=[kC])
                    P.op("pe", lambda e, bC=bC: e.matmul(bC[:, 128:256], IDN, MUP, start=False, stop=True), reads=[("cst",)], writes=[kC])
                    P.op("pe", lambda e, p=p, e_=e_, bC=bC: e.matmul(bC[:, 256:384], ONES, Rd[p][e_], start=True, stop=False), reads=[("cst",), ("Rd", p, e_)], writes=[kC])
                    P.op("pe", lambda e, bC=bC: e.matmul(bC[:, 256:384], IDN, MLO, start=False, stop=True), reads=[("cst",)], writes=[kC])
                    P.op("act", lambda e, p=p, e_=e_, cl=cl, bC=bC: e.activation(out=E2[p][e_], in_=bC[:, 128:256], func=AF.Exp, bias=ngc[:, cl:cl + 1], scale=1.0),
                         reads=[kC, ("ngc",)], writes=[("E2", p, e_)])
                    P.op("act", lambda e, p=p, e_=e_, cl=cl, bC=bC: e.activation(out=E2T[p][e_], in_=bC[:, 256:384], func=AF.Exp, bias=gc[:, cl:cl + 1], scale=-1.0),
                         reads=[kC, ("gc",)], writes=[("E2T", p, e_)])
                    P.op("act", lambda e, p=p, e_=e_, bC=bC: e.activation(out=EG[p][e_], in_=bC[:, 0:128], func=AF.Exp), reads=[kC], writes=[("EG", p, e_)])
                    P.op("dve", lambda e, p=p, e_=e_, cl=cl: e.scalar_tensor_tensor(out=Mt[p][e_], in0=KQ[p][:, 0:128], scalar=nbeta[:, cl:cl + 1], in1=E2T[p][e_], op0=ALU.mult, op1=ALU.mult),
                         reads=[("KQ", p), ("nbeta",), ("E2T", p, e_)], writes=[("Mt", p, e_)])
                    P.op(POOLENG, lambda e, p=p, e_=e_: e.tensor_tensor(out=QKd[p][e_], in0=KQ[p][:, 128:256], in1=E2[p][e_], op=ALU.mult),
                         reads=[("KQ", p), ("E2", p, e_)], writes=[("QKd", p, e_)])
                    P.op(POOLENG, lambda e, p=p, e_=e_, cs=cs: e.tensor_tensor(out=QgT[p][e_], in0=qT[:, cs], in1=EG[p][e_], op=ALU.mult),
                         reads=[("qT",), ("EG", p, e_)], writes=[("QgT", p, e_)])
                if os.environ.get("KDBG", "") == "c1":
                    continue
                bC = [bank[2 + e_] for e_ in range(2)]; kC = [("ps", 2 + e_) for e_ in range(2)]
                bE = [bank[4 + e_] for e_ in range(2)]; kE = [("ps", 4 + e_) for e_ in range(2)]
                bF = [bank[6 + e_] for e_ in range(2)]; kF = [("ps", 6 + e_) for e_ in range(2)]
                for e_ in range(2):
                    P.op("pe", lambda e, p=p, e_=e_: e.matmul(bE[e_][:, 256:384], Mt[p][e_], IDN, start=True, stop=True), reads=[("Mt", p, e_), ("cst",)], writes=[kE[e_]])
                for e_ in range(2):
                    P.op("dve", lambda e, p=p, e_=e_: e.tensor_copy(out=MP[p][e_][:, 0:128], in_=bE[e_][:, 256:384]), reads=[kE[e_]], writes=[("M", p, e_)])
                    P.op("dve", lambda e, p=p, e_=e_: e.tensor_tensor(out=MP[p][e_][:, 128:256], in0=MP[p][e_][:, 0:128], in1=IDN, op=ALU.add),
                         reads=[("M", p, e_), ("cst",)], writes=[("Pm", p, e_)])
                for e_ in range(2):
                    P.op("pe", lambda e, p=p, e_=e_: e.matmul(bE[e_][:, 0:128], Mt[p][e_], MP[p][e_][:, 0:128], start=True, stop=True), reads=[("Mt", p, e_), ("M", p, e_)], writes=[kE[e_]])
                    P.op("pe", lambda e, p=p, e_=e_: e.matmul(bC[e_][:, 384:512], MP[p][e_][:, 0:128], Mt[p][e_], start=True, stop=True), reads=[("Mt", p, e_), ("M", p, e_)], writes=[kC[e_]])
                for e_ in range(2):
                    P.op("dve", lambda e, p=p, e_=e_: e.tensor_copy(out=MP[p][e_][:, 0:128], in_=bE[e_][:, 0:128]), reads=[kE[e_]], writes=[("M", p, e_)])
                    P.op("act", lambda e, p=p, e_=e_: e.copy(out=Mt[p][e_], in_=bC[e_][:, 384:512]), reads=[kC[e_]], writes=[("Mt", p, e_)])
                for k in range(1, 7):
                    for e_ in range(2):
                        if k < 6:
                            P.op("pe", lambda e, p=p, e_=e_: e.matmul(bE[e_][:, 0:256], Mt[p][e_], MP[p][e_], start=True, stop=True),
                                 reads=[("Mt", p, e_), ("M", p, e_), ("Pm", p, e_)], writes=[kE[e_]])
                            P.op("pe", lambda e, p=p, e_=e_: e.matmul(bC[e_][:, 384:512], MP[p][e_][:, 0:128], Mt[p][e_], start=True, stop=True),
                                 reads=[("Mt", p, e_), ("M", p, e_)], writes=[kC[e_]])
                        else:
                            P.op("pe", lambda e, p=p, e_=e_: e.matmul(bE[e_][:, 128:256], Mt[p][e_], MP[p][e_][:, 128:256], start=True, stop=True),
                                 reads=[("Mt", p, e_), ("Pm", p, e_)], writes=[kE[e_]])
                    for e_ in range(2):
                        if k < 5:
                            P.op("dve", lambda e, p=p, e_=e_: e.tensor_copy(out=MP[p][e_][:, 0:128], in_=bE[e_][:, 0:128]), reads=[kE[e_]], writes=[("M", p, e_)])
                        if k < 6:
                            P.op("dve", lambda e, p=p, e_=e_: e.tensor_tensor(out=MP[p][e_][:, 128:256], in0=bE[e_][:, 128:256], in1=MP[p][e_][:, 128:256], op=ALU.add),
                                 reads=[kE[e_], ("Pm", p, e_)], writes=[("Pm", p, e_)])
                            P.op("act", lambda e, p=p, e_=e_: e.copy(out=Mt[p][e_], in_=bC[e_][:, 384:512]), reads=[kC[e_]], writes=[("Mt", p, e_)])
                        else:
                            P.op("dve", lambda e, p=p, e_=e_: e.tensor_tensor(out=TTt[p][e_], in0=bE[e_][:, 128:256], in1=MP[p][e_][:, 128:256], op=ALU.add),
                                 reads=[kE[e_], ("Pm", p, e_)], writes=[("TT", p, e_)])
                if os.environ.get("KDBG", "") == "c2":
                    continue
                for e_ in range(2):
                    P.op("pe", lambda e, e_=e_, cs=cs: e.matmul(bF[e_][:, 0:128], kT[:, cs], Sbf[e_], start=True, stop=False), reads=[("kT",), ("Sbf", e_)], writes=[kF[e_]])
                    P.op("pe", lambda e, e_=e_, cs=cs: e.matmul(bF[e_][:, 0:128], kT[:, cs], Slo[e_], start=False, stop=True), reads=[("kT",), ("Slo", e_)], writes=[kF[e_]])
                for e_ in range(2):
                    cl = col[e_]
                    P.op("dve", lambda e, p=p, e_=e_, cl=cl: e.scalar_tensor_tensor(out=Dbf[p][e_], in0=bF[e_][:, 0:128], scalar=nbg[:, cl:cl + 1], in1=Vb[p][e_], op0=ALU.mult, op1=ALU.add),
                         reads=[kF[e_], ("bg",), ("Vb", p, e_)], writes=[("Dbf", p, e_)])
                for e_ in range(2):
                    P.op("pe", lambda e, p=p, e_=e_: e.matmul(bC[e_][:, 128:256], TTt[p][e_], Dbf[p][e_], start=True, stop=True), reads=[("TT", p, e_), ("Dbf", p, e_)], writes=[kC[e_]])
                for e_ in range(2):
                    P.op("act", lambda e, p=p, e_=e_: e.copy(out=vnew[p][e_], in_=bC[e_][:, 128:256]), reads=[kC[e_]], writes=[("vnew", p, e_)])
                for e_ in range(2):
                    P.op("pe", lambda e, p=p, e_=e_: e.matmul(bC[e_][:, 0:128], Sbf[e_], QgT[p][e_], start=True, stop=False), reads=[("Sbf", e_), ("QgT", p, e_)], writes=[kC[e_]])
                    P.op("pe", lambda e, p=p, e_=e_: e.matmul(bC[e_][:, 0:128], vnew[p][e_], QKd[p][e_], start=False, stop=True), reads=[("vnew", p, e_), ("QKd", p, e_)], writes=[kC[e_]])
                    P.op("pe", lambda e, p=p, e_=e_: e.matmul(bE[e_][:, 384:512], Kd[p][e_], vnew[p][e_], start=True, stop=True), reads=[("Kd", p, e_), ("vnew", p, e_)], writes=[kE[e_]])
                for e_ in range(2):
                    cl = col[e_]
                    P.op("act", lambda e, e_=e_, cs=cs: e.copy(out=oT[e_][:, cs], in_=bC[e_][:, 0:128]), reads=[kC[e_]], writes=[("oT", e_, c)])
                    P.op("dve", lambda e, e_=e_, cl=cl: e.scalar_tensor_tensor(out=S32[e_], in0=S32[e_], scalar=egl[:, cl:cl + 1], in1=bE[e_][:, 384:512], op0=ALU.mult, op1=ALU.add),
                         reads=[("S32", e_), ("egl",), kE[e_]], writes=[("S32", e_)])
                    P.op("act", lambda e, e_=e_: e.copy(out=Sbf[e_], in_=S32[e_]), reads=[("S32", e_)], writes=[("Sbf", e_)])
                    P.op(POOLENG, lambda e, e_=e_: e.tensor_tensor(out=Slo[e_], in0=S32[e_], in1=Sbf[e_], op=ALU.subtract), reads=[("S32", e_), ("Sbf", e_)], writes=[("Slo", e_)])
            if os.environ.get("KDBG", "") in ("chunk", "c1", "c2", "c3"):
                P.barrier(); sb.release(m)
                return
            for e_ in range(2):
                hv = 2 * g + e_
                o = oT[e_]
                okeys = [("oT", e_, c) for c in range(NC)]
                for tt in range(NT):
                    i = nsq % 2; nsq += 1
                    sl = slice(tt * TT, (tt + 1) * TT)
                    bk = 6 + i
                    kk = [("ps", bk, r) for r in range(4)]
                    tkeys = [("oT", e_, c) for c in range(tt * 4, tt * 4 + 4)]
                    P.op("act", lambda e, i=i, sl=sl, o=o: e.activation(out=sqb[i], in_=o[:, sl], func=AF.Square), reads=tkeys, writes=[("sqb", i)])
                    P.op("pe", lambda e, i=i, bk=bk: e.matmul(bank[bk], self.ones128_bf, sqb[i], start=True, stop=True), reads=[("sqb", i), ("cbf", 3)], writes=kk)
                    P.op("act", lambda e, i=i, bk=bk: e.activation(out=rnb[i], in_=bank[bk], func=AF.Sqrt, bias=self.eps_ap, scale=1.0), reads=kk + [("cbf", 4)], writes=[("rnb", i)])
                    P.op("dve", lambda e, i=i: e.reciprocal(out=rnb[i], in_=rnb[i]), reads=[("rnb", i)], writes=[("rnb", i)])
                    zi = nz % 2; nz += 1
                    zc = 64 + hv
                    P.dma("sp", "dzb%d" % zi, lambda e, zi=zi, zc=zc, sl=sl: e.dma_start(out=zb[zi], in_=projv[:, zc, sl]), reads=[("proj", zc)], writes=[("dzb", zi)])
                    P.op("act", lambda e, zi=zi: e.activation(out=zb[zi], in_=zb[zi], func=AF.Silu), reads=[("dzb", zi)], writes=[("dzb", zi)])
                    P.op("dve", lambda e, i=i, sl=sl, o=o: e.tensor_tensor(out=o[:, sl], in0=o[:, sl], in1=rnb[i], op=ALU.mult), reads=tkeys + [("rnb", i)], writes=tkeys)
                    P.op("dve", lambda e, zi=zi, sl=sl, o=o, e_=e_: e.scalar_tensor_tensor(out=ost[e_][:, sl], in0=o[:, sl], scalar=dng[:, 0:1], in1=zb[zi], op0=ALU.mult, op1=ALU.mult),
                         reads=tkeys + [("dng",), ("dzb", zi)], writes=[("ost", e_, tt)])
                P.dma("sp", "dost%d" % e_, lambda e, e_=e_, hv=hv: e.dma_start(out=self.mixed[hv * 128:(hv + 1) * 128, :], in_=ost[e_]),
                      reads=[("ost", e_, tt) for tt in range(NT)], writes=[("mixed", hv)])
        P.barrier()
        sb.release(m)
        m = sb.mark()
        mix = MixStage(self)
        zb2 = [sb.alloc((8, TT), BF16) for _ in range(2)]
        self.alloc_attention()
        nz = 0
        for tt in range(NT):
            zstate = {}

            def sz_loader(c, tt=tt, zstate=zstate):
                nonlocal nz
                grp = c // 8
                if grp not in zstate:
                    zt = zb2[nz % 2]; slot = nz % 2; nz += 1
                    P.dma("sp", "zb%d" % slot, lambda e, zt=zt, grp=grp: e.dma_start(out=zt, in_=projv[:, 64 + grp * 8:64 + grp * 8 + 8, tt * TT:(tt + 1) * TT]),
                          reads=[("proj", 64 + grp * 8 + i) for i in range(8)], writes=[("zb", slot)])
                    P.op("act", lambda e, zt=zt: e.activation(out=zt, in_=zt, func=AF.Silu), reads=[("zb", slot)], writes=[("zb", slot)])
                    zstate[grp] = (zt, slot)
                zt, slot = zstate[grp]
                return zt[:, c % 8, :], ("zb", slot)

            self.attention_tile(li, tt, 8192 + MIX, sz_loader, mix)
        P.barrier()
        sb.release(m)

    def phase_outproj(self, li):
        P = self.P; sb = self.sb; NT = self.NT
        m = sb.mark()
        mixv = self.mixed.rearrange("(c p) s -> p c s", p=128)
        hTv = self.hT.rearrange("(kc p) s -> p kc s", p=128)
        Wv = self.w_out[li].rearrange("(kc p) n -> p kc n", p=128)
        mx = [sb.alloc((48, TT), BF16) for _ in range(2)]
        wb = [sb.alloc((8, 512), BF16) for _ in range(4)]
        hb = [sb.alloc((4, TT), F32) for _ in range(2)]
        nw = 0
        nh = 0
        for tt in range(NT):
            mt = mx[tt % 2]
            P.dma("sp", "mx%d" % (tt % 2), lambda e, mt=mt, tt=tt: e.dma_start(out=mt, in_=mixv[:, :, tt * TT:(tt + 1) * TT]),
                  reads=[("mixed", tt)], writes=[("mx", tt % 2)])
            for q in range(4):
                h = hb[nh % 2]; hs = nh % 2; nh += 1
                P.dma("sp", "ohb%d" % hs, lambda e, h=h, q=q, tt=tt: e.dma_start(out=h, in_=hTv[:, q * 4:(q + 1) * 4, tt * TT:(tt + 1) * TT]),
                      reads=[("hT", tt, q)], writes=[("ohb", hs)])
                banks = [(q % 2) * 4 + r for r in range(4)]
                for kg in range(6):
                    w = wb[nw % 4]; ws = nw % 4; nw += 1
                    P.dma("pool", "wo%d" % ws, lambda e, w=w, kg=kg, q=q: e.dma_start(out=w, in_=Wv[:, kg * 8:(kg + 1) * 8, q * 512:(q + 1) * 512]),
                          writes=[("wo", ws)])
                    for r in range(4):
                        bank = self.bank[banks[r]]
                        for k8 in range(8):
                            kc = kg * 8 + k8
                            P.op("pe", lambda e, bank=bank, w=w, k8=k8, r=r, mt=mt, kc=kc: e.matmul(bank, w[:, k8, r * 128:(r + 1) * 128], mt[:, kc, :], start=(kc == 0), stop=(kc == 47)),
                                 reads=[("wo", ws), ("mx", tt % 2)], writes=[("ps", banks[r])])
                for r in range(4):
                    bank = self.bank[banks[r]]
                    P.op("dve", lambda e, h=h, r=r, bank=bank: e.tensor_tensor(out=h[:, r, :], in0=bank, in1=h[:, r, :], op=ALU.add),
                         reads=[("ps", banks[r]), ("ohb", hs)], writes=[("ohb", hs)])
                P.dma("sp", "ohb%d" % hs, lambda e, h=h, q=q, tt=tt: e.dma_start(out=hTv[:, q * 4:(q + 1) * 4, tt * TT:(tt + 1) * TT], in_=h),
                      reads=[("ohb", hs)], writes=[("hT", tt, q)])
        P.barrier()
        sb.release(m)

    def phase_final(self):
        P = self.P; sb = self.sb; S = self.S
        m = sb.mark()
        gbc = sb.alloc((D,), F32)
        P.dma("sp", "gbc", lambda e: e.dma_start(out=gbc, in_=self.fing.partition_broadcast(128)), writes=[("gbc",)])
        hTv = self.hT.rearrange("(kc p) s -> p kc s", p=128)
        hin = [sb.alloc((NKC, 128), F32) for _ in range(2)]
        xt = [sb.alloc((D,), F32) for _ in range(2)]
        ot = [sb.alloc((D,), F32) for _ in range(2)]
        junk = sb.alloc((D,), BF16)
        ss = sb.alloc((4,), F32)
        for t in range(S // 128):
            i = t % 2
            hi = hin[i]; x = xt[i]; o = ot[i]
            P.dma("sp", "hin%d" % i, lambda e, hi=hi, t=t: e.dma_start(out=hi, in_=hTv[:, :, t * 128:(t + 1) * 128]), writes=[("hin", i)])
            for q in range(4):
                bk = (t * 4 + q) % 8
                bank = self.bank[bk]
                for r in range(4):
                    kc = q * 4 + r
                    P.op("pe", lambda e, bank=bank, r=r, hi=hi, kc=kc: e.transpose(out=bank[:, r * 128:(r + 1) * 128], in_=hi[:, kc, :], identity=self.ident),
                         reads=[("hin", i), ("cst",)], writes=[("ps", bk)])
                dst = x[:, q * 512:(q + 1) * 512]
                if q % 2 == 0:
                    P.op("dve", lambda e, dst=dst, bank=bank: e.tensor_copy(out=dst, in_=bank), reads=[("ps", bk)], writes=[("xt", i, q)])
                else:
                    P.op("act", lambda e, dst=dst, bank=bank: e.copy(out=dst, in_=bank), reads=[("ps", bk)], writes=[("xt", i, q)])
            s1 = ss[:, i * 2:i * 2 + 1]
            s2 = ss[:, i * 2 + 1:i * 2 + 2]
            xk = [("xt", i, q) for q in range(4)]
            P.op("act", lambda e, x=x, s1=s1: e.activation(out=junk, in_=x, func=AF.Square, accum_out=s1), reads=xk, writes=[("fss", i), ("junk",)])
            P.op("act", lambda e, s1=s1, s2=s2: e.activation(out=s2, in_=s1, func=AF.Sqrt, bias=self.eps_ap, scale=1.0 / D),
                 reads=[("fss", i), ("cbf", 4)], writes=[("fss2", i)])
            P.op("dve", lambda e, s2=s2: e.reciprocal(out=s2, in_=s2), reads=[("fss2", i)], writes=[("fss2", i)])
            P.op("dve", lambda e, o=o, x=x, s2=s2: e.scalar_tensor_tensor(out=o, in0=x, scalar=s2, in1=gbc, op0=ALU.mult, op1=ALU.mult),
                 reads=xk + [("fss2", i), ("gbc",)], writes=[("ot", i)])
            P.dma("sp", "ot%d" % i, lambda e, o=o, t=t: e.dma_start(out=self.out[t * 128:(t + 1) * 128, :], in_=o), reads=[("ot", i)], writes=[("out", t)])
        P.barrier()
        sb.release(m)


LAYERS_FULL = [("pool", 0, 0), ("delta", 1, 0), ("pool", 2, 1), ("delta", 3, 1)]


def chunk_cols(v):
    return np.ascontiguousarray(np.asarray(v, np.float32).reshape(-1, 128).T)


def prepare_shared(inputs, layers, S):
    pool_js = sorted({j for k, l, j in layers if k == "pool"}) or [0]
    delta_js = sorted({j for k, l, j in layers if k == "delta"}) or [0]
    ls = [l for k, l, j in layers]
    f = lambda a: np.ascontiguousarray(np.asarray(a, np.float32))
    sh = {}
    sh["consts"] = make_consts()
    sh["lng"] = np.concatenate([chunk_cols(inputs["layer_norm_g"][l]) for l in ls], axis=1)
    sh["memg"] = f(inputs["mem_norm_g"])
    sh["fing"] = f(inputs["final_norm_g"])
    sh["w_in_pool"] = f(np.asarray(inputs["w_in_pool"])[pool_js])
    sh["pool_maps"] = f(np.asarray(inputs["pool_maps"])[pool_js])
    sh["pool_scale"] = np.concatenate([chunk_cols(inputs["pool_scale"][j]) for j in pool_js], axis=1)
    sh["w_in_delta"] = f(np.asarray(inputs["w_in_delta"])[delta_js])
    cw = []
    for j in delta_js:
        w = np.asarray(inputs["dn_conv_w"][j], np.float32)
        cw.append(np.ascontiguousarray(w.reshape(4, 64, 128).transpose(2, 1, 0)).reshape(128, 256))
    sh["conv_w"] = np.ascontiguousarray(np.concatenate(cw, axis=1))
    NC = S // 128
    sh["a_log_rep"] = np.ascontiguousarray(np.stack([np.tile(np.asarray(inputs["dn_a_log"][j], np.float32)[None, :], (128, NC)) for j in delta_js]))
    sh["dt_bias_rep"] = np.ascontiguousarray(np.stack([np.tile(np.asarray(inputs["dn_dt_bias"][j], np.float32)[None, :], (128, NC)) for j in delta_js]))
    sh["dn_g"] = np.ascontiguousarray(np.stack([np.asarray(inputs["dn_norm_g"][j], np.float32) for j in delta_js], axis=0)[:, :, None])
    sh["w_kv"] = f(np.asarray(inputs["w_mem_kv"])[ls])
    sh["w_out"] = f(np.asarray(inputs["w_out"])[ls])
    return sh


def remap_layers(layers):
    pool_js = sorted({j for k, l, j in layers if k == "pool"}) or [0]
    delta_js = sorted({j for k, l, j in layers if k == "delta"}) or [0]
    out = []
    for k, l, j in layers:
        out.append((k, l, pool_js.index(j) if k == "pool" else delta_js.index(j)))
    return out


def run(inputs, layers=LAYERS_FULL, n_cores=8, S=4096, trace=False):
    sh = prepare_shared(inputs, layers, S)
    b = Builder(S, remap_layers(layers))
    nc = b.build()
    x = np.asarray(inputs["x"], np.float32)
    mem = np.asarray(inputs["mem"], np.float32)
    in_maps = []
    for c in range(n_cores):
        d = dict(sh)
        d["x"] = np.ascontiguousarray(x[c, :S])
        d["mem"] = np.ascontiguousarray(mem[c])
        in_maps.append(d)
    res = run_bass_kernel_spmd(nc, in_maps, core_ids=list(range(n_cores)), trace=trace)
    out = np.stack([np.asarray(r["out"]) for r in res.results], axis=0)
    return out, res


def kernel(**inputs):
    out, _ = run(inputs)
    return out.astype(np.float32)
```

```python
import os
import numpy as np
import concourse.bass as bass
import concourse.mybir as mybir
from concourse.bass_utils import run_bass_kernel_spmd
from contextlib import ExitStack

F32 = mybir.dt.float32
BF16 = mybir.dt.bfloat16
AF = mybir.ActivationFunctionType
ALU = mybir.AluOpType
AX = mybir.AxisListType

D = 2048
NKC = 16
MEM = 256
BR = 4096
MIX = 6144
XA = 2048
POOL_IN = 12288
DELTA_IN = 16448
EPS = 1e-6
TT = 512
SEM_LIMIT = 30000
POOLENG = os.environ.get('KPOOL', 'dve')


class Op:
    __slots__ = ("eng", "fn", "sem", "inc", "deps", "val", "epoch", "marked", "is_dma")


class Prog:
    ENGS = ("pe", "act", "dve", "pool", "sp")

    def __init__(self, nc):
        self.nc = nc
        self.ops = []
        self.last_w = {}
        self.readers = {}
        self.last_on_sem = {}
        self.ps_acc = {}
        self.barrier_deps = None
        self.passed = {e: True for e in self.ENGS}

    def _add(self, eng, fn, reads, writes, sem, inc, is_dma):
        idx = len(self.ops)
        op = Op()
        op.eng = eng; op.fn = fn; op.sem = sem; op.inc = inc; op.is_dma = is_dma
        op.val = 0; op.epoch = 0; op.marked = False
        deps = set()
        ops = self.ops
        reads = [k[:2] if k[0] == "ps" else k for k in reads]
        writes = [k[:2] if k[0] == "ps" else k for k in writes]
        for k in reads:
            w = self.last_w.get(k)
            if w is not None:
                o = ops[w]
                if o.is_dma or is_dma or o.eng != eng or eng != "pe":
                    deps.add(w)
        for k in writes:
            w = self.last_w.get(k)
            if w is not None:
                o = ops[w]
                if o.is_dma or is_dma or o.eng != eng or eng != "pe":
                    deps.add(w)
            rd = self.readers.get(k)
            if rd:
                for r in rd.values():
                    o = ops[r]
                    if o.is_dma or is_dma or o.eng != eng or eng != "pe":
                        deps.add(r)
        for k in set(reads) | set(writes):
            if k[0] == "ps":
                acc = self.ps_acc.setdefault(k, {})
                for en, d in acc.items():
                    if en != eng:
                        deps.add(d)
                acc[eng] = idx
        if is_dma:
            p = self.last_on_sem.get(sem)
            if p is not None:
                deps.add(p)
        if not self.passed[eng]:
            self.passed[eng] = True
            for s, d in self.barrier_deps.items():
                o = ops[d]
                if o.is_dma or o.eng != eng:
                    deps.add(d)
        op.deps = deps
        for k in reads:
            self.readers.setdefault(k, {})[sem] = idx
        for k in writes:
            self.last_w[k] = idx
            self.readers[k] = {}
        self.last_on_sem[sem] = idx
        ops.append(op)
        return idx

    def op(self, eng, fn, reads=(), writes=()):
        return self._add(eng, fn, reads, writes, eng, 1, False)

    def dma(self, eng, slot, fn, reads=(), writes=()):
        return self._add(eng, fn, reads, writes, "dma:" + slot, 16, True)

    def barrier(self):
        self.barrier_deps = dict(self.last_on_sem)
        self.passed = {e: False for e in self.ENGS}
        self.last_w = {}
        self.readers = {}
        self.ps_acc = {}

    def emit(self):
        nc = self.nc
        ops = self.ops
        final_deps = set(self.last_on_sem.values())
        for op in ops:
            for d in op.deps:
                ops[d].marked = True
        for d in final_deps:
            ops[d].marked = True
        cnt = {}
        ep = {}
        semnames = set()
        for op in ops:
            if not op.marked:
                continue
            c = cnt.get(op.sem, 0)
            e = ep.get(op.sem, 0)
            if c + op.inc > SEM_LIMIT:
                e += 1
                c = 0
            c += op.inc
            cnt[op.sem] = c
            ep[op.sem] = e
            op.val = c
            op.epoch = e
            semnames.add((op.sem, e))
        with ExitStack() as st:
            sems = {}
            for i, key in enumerate(sorted(semnames)):
                sems[key] = st.enter_context(nc.semaphore("s%d" % i))
            block = st.enter_context(nc.Block())

            def run(engname, e):
                waited = {}
                for op in ops:
                    if op.eng != engname:
                        continue
                    for d in sorted(op.deps):
                        o = ops[d]
                        w = waited.get(o.sem)
                        if w is not None and (w[0] > o.epoch or (w[0] == o.epoch and w[1] >= o.val)):
                            continue
                        e.wait_ge(sems[(o.sem, o.epoch)], o.val)
                        waited[o.sem] = (o.epoch, o.val)
                    ins = op.fn(e)
                    if op.marked:
                        ins.then_inc(sems[(op.sem, op.epoch)], op.inc)
                if engname == "sp":
                    for d in sorted(final_deps):
                        o = ops[d]
                        w = waited.get(o.sem)
                        if w is not None and (w[0] > o.epoch or (w[0] == o.epoch and w[1] >= o.val)):
                            continue
                        e.wait_ge(sems[(o.sem, o.epoch)], o.val)
                        waited[o.sem] = (o.epoch, o.val)

            @block.tensor
            def _(e):
                run("pe", e)

            @block.scalar
            def _(e):
                run("act", e)

            @block.vector
            def _(e):
                run("dve", e)

            @block.gpsimd
            def _(e):
                run("pool", e)

            @block.sync
            def _(e):
                run("sp", e)


class SB:
    def __init__(self, big, nwords):
        self.big = big
        self.cap = nwords * 4
        self.off = 0

    def alloc(self, shape, dtype, parts=128):
        n = 1
        for s in shape:
            n *= s
        nb = n * (4 if dtype == F32 else 2)
        nb = (nb + 63) // 64 * 64
        assert self.off + nb <= self.cap, ("SBUF overflow", self.off, nb, self.cap)
        w = self.big[:, self.off // 4:(self.off + nb) // 4]
        self.off += nb
        ap = w if dtype == F32 else w.bitcast(BF16)
        ap = ap[:, 0:n]
        if len(shape) == 2:
            ap = ap.rearrange("p (a b) -> p a b", a=shape[0])
        elif len(shape) == 3:
            ap = ap.rearrange("p (a b c) -> p a b c", a=shape[0], b=shape[1])
        if parts != 128:
            ap = ap[0:parts]
        return ap

    def mark(self):
        return self.off

    def release(self, m):
        self.off = m


C_ID = 0
C_ONE = 128
C_UT = 256
C_SEL = 384
C_MUP = 512
C_MLO = 640
C_INV = 768
NCONST = 768 + 512


def make_consts():
    c = np.zeros((128, NCONST), np.float32)
    c[:, C_ID:C_ID + 128] = np.eye(128, dtype=np.float32)
    c[:, C_ONE:C_ONE + 128] = 1.0
    r = np.arange(128)[:, None]
    q = np.arange(128)[None, :]
    c[:, C_UT:C_UT + 128] = (r <= q).astype(np.float32)
    c[:, C_SEL:C_SEL + 128] = (r == 127).astype(np.float32) * np.ones((1, 128), np.float32)
    c[:, C_MUP:C_MUP + 128] = np.where(q >= r, 0.0, -30000.0)
    c[:, C_MLO:C_MLO + 128] = np.where(q < r, 0.0, 30000.0)
    for g, w in enumerate((2, 4, 8, 16)):
        inv = 1.0 / np.minimum(np.arange(1, 17), w).astype(np.float32)
        c[:, C_INV + g * 128:C_INV + (g + 1) * 128] = np.tile(inv, 8)[None, :]
    return c


class MixStage:
    def __init__(self, B):
        self.B = B
        self.bufs = [B.sb.alloc((8, TT), BF16) for _ in range(2)]
        self.n = 0
        self.cur = None

    def begin(self, G):
        slot = self.n % 2
        self.n += 1
        self.cur = (self.bufs[slot], slot, G)

    def dst(self, c):
        buf, slot, G = self.cur
        assert c // 8 == G
        return buf[:, c % 8, :], ("mxs", slot, c % 8)

    def flush(self, tt):
        buf, slot, G = self.cur
        B = self.B
        mixv = B.mixed.rearrange("(c p) s -> p c s", p=128)
        B.P.dma("sp", "mxs%d" % slot, lambda e: e.dma_start(out=mixv[:, G * 8:(G + 1) * 8, tt * TT:(tt + 1) * TT], in_=buf),
                reads=[("mxs", slot, i) for i in range(8)], writes=[("mixed", tt, G)])


class Builder:
    def __init__(self, S, layers, debug_out=None):
        self.S = S
        self.NT = S // TT
        self.layers = layers
        nc = bass.Bass("TRN2", target_bir_lowering=False)
        self.nc = nc
        self.P = Prog(nc)
        dt = nc.dram_tensor
        n_pool = max(1, sum(1 for k in layers if k[0] == "pool"))
        n_delta = max(1, sum(1 for k in layers if k[0] == "delta"))
        nl = len(layers)
        self.nl = nl
        I = "ExternalInput"
        self.x = dt("x", [S, D], F32, kind=I).ap()
        self.mem = dt("mem", [MEM, D], F32, kind=I).ap()
        self.consts = dt("consts", [128, NCONST], F32, kind=I).ap()
        self.lng = dt("lng", [128, nl * NKC], F32, kind=I).ap()
        self.memg = dt("memg", [D], F32, kind=I).ap()
        self.fing = dt("fing", [D], F32, kind=I).ap()
        self.w_in_pool = dt("w_in_pool", [n_pool, D, POOL_IN], F32, kind=I).ap()
        self.pool_maps = dt("pool_maps", [n_pool, 4, 1024, 1024], F32, kind=I).ap()
        self.pool_scale = dt("pool_scale", [128, n_pool * 32], F32, kind=I).ap()
        self.w_in_delta = dt("w_in_delta", [n_delta, D, DELTA_IN], F32, kind=I).ap()
        self.conv_w = dt("conv_w", [128, n_delta * 64 * 4], F32, kind=I).ap()
        self.a_log_rep = dt("a_log_rep", [n_delta, 128, (S // 128) * 32], F32, kind=I).ap()
        self.dt_bias_rep = dt("dt_bias_rep", [n_delta, 128, (S // 128) * 32], F32, kind=I).ap()
        self.dn_g = dt("dn_g", [n_delta, 128, 1], F32, kind=I).ap()
        self.w_kv = dt("w_kv", [nl, D, 2 * XA], F32, kind=I).ap()
        self.w_out = dt("w_out", [nl, MIX, D], F32, kind=I).ap()
        self.out = dt("out", [S, D], F32, kind="ExternalOutput").ap()
        self.hT = dt("hT", [D, S], F32).ap()
        self.proj = dt("proj", [DELTA_IN - 64, S], BF16).ap()
        self.mixed = dt("mixed", [MIX, S], BF16).ap()
        self.abt = dt("abt", [S, 64], F32).ap()

    def build(self):
        nc = self.nc
        P = self.P
        with ExitStack() as st:
            NW = 52992
            big = st.enter_context(nc.sbuf_tensor("big", [128, NW], F32))
            ps = st.enter_context(nc.psum_tensor("ps", [128, 4096], F32))
            self.sb = SB(big, NW)
            self.ps = ps
            self.bank = [ps[:, i * 512:(i + 1) * 512] for i in range(8)]
            self.setup_consts()
            self.phase_x0()
            self.phase_mem()
            for li, (kind, l, j) in enumerate(self.layers):
                self.phase_kv(li)
                if kind == "pool":
                    self.phase_inproj(li, self.w_in_pool[j], POOL_IN)
                    self.phase_pool_mix(li, j)
                else:
                    self.phase_inproj(li, self.w_in_delta[j], DELTA_IN)
                    if os.environ.get("KDBG", "") == "inproj":
                        break
                    self.phase_delta_mix(li, j)
                if os.environ.get("KDBG", ""):
                    break
                self.phase_outproj(li)
            self.phase_final()
            P.emit()
        return nc

    def setup_consts(self):
        P = self.P
        sb = self.sb
        self.cst = sb.alloc((NCONST,), F32)
        cst = self.cst
        P.dma("sp", "cst", lambda e: e.dma_start(out=cst, in_=self.consts), writes=[("cst",)])
        self.ident = cst[:, C_ID:C_ID + 128]
        self.ident_bf = sb.alloc((128,), BF16)
        self.onesD_bf = sb.alloc((128,), BF16)
        self.ones_bf = sb.alloc((128,), BF16)
        self.ones128_bf = sb.alloc((128,), BF16)
        ib, od, ob, o128 = self.ident_bf, self.onesD_bf, self.ones_bf, self.ones128_bf
        P.op("dve", lambda e: e.tensor_copy(out=ib, in_=cst[:, C_ID:C_ID + 128]), reads=[("cst",)], writes=[("cbf", 0)])
        P.op("dve", lambda e: e.tensor_scalar(out=od, in0=cst[:, C_ONE:C_ONE + 128], scalar1=1.0 / D, scalar2=None, op0=ALU.mult),
             reads=[("cst",)], writes=[("cbf", 1)])
        P.op("dve", lambda e: e.tensor_copy(out=ob, in_=cst[:, C_ONE:C_ONE + 128]), reads=[("cst",)], writes=[("cbf", 2)])
        P.op("dve", lambda e: e.tensor_scalar(out=o128, in0=cst[:, C_ONE:C_ONE + 128], scalar1=1.0 / 128, scalar2=None, op0=ALU.mult),
             reads=[("cst",)], writes=[("cbf", 3)])
        self.eps_ap = sb.alloc((1,), F32)
        ea = self.eps_ap
        P.op("dve", lambda e: e.memset(ea, EPS), writes=[("cbf", 4)])
        nl = self.nl
        self.lng_sb = sb.alloc((nl * NKC,), F32)
        lg = self.lng_sb
        P.dma("sp", "cst", lambda e: e.dma_start(out=lg, in_=self.lng), writes=[("lng",)])
        self.mem_nT = sb.alloc((NKC, MEM), BF16)
        self.KT = sb.alloc((NKC, MEM), BF16)
        self.V = sb.alloc((2, XA), BF16)
        self.persist_mark = sb.mark()

    def phase_x0(self):
        P = self.P; sb = self.sb
        m = sb.mark()
        xin = [sb.alloc((D,), F32) for _ in range(2)]
        hst = [sb.alloc((NKC, TT), F32) for _ in range(2)]
        hTv = self.hT.rearrange("(kc p) s -> p kc s", p=128)
        ident = self.ident
        n = 0
        for tt in range(self.NT):
            hs = hst[tt % 2]
            for ts in range(4):
                xb = xin[n % 2]
                tok0 = tt * TT + ts * 128
                P.dma("sp", "xin%d" % (n % 2), lambda e, xb=xb, tok0=tok0: e.dma_start(out=xb, in_=self.x[tok0:tok0 + 128, :]),
                      writes=[("xin", n % 2)])
                for q in range(4):
                    bk = (n * 4 + q) % 8
                    bank = self.bank[bk]
                    for r in range(4):
                        kc = q * 4 + r
                        P.op("pe", lambda e, bank=bank, r=r, xb=xb, kc=kc: e.transpose(out=bank[:, r * 128:(r + 1) * 128], in_=xb[:, kc * 128:(kc + 1) * 128], identity=ident),
                             reads=[("xin", n % 2), ("cst",)], writes=[("ps", bk)])
                    dst = hs[:, q * 4:(q + 1) * 4, ts * 128:(ts + 1) * 128]
                    src = bank.rearrange("p (a b) -> p a b", a=4)
                    eng = "dve" if q % 2 == 0 else "act"
                    if eng == "dve":
                        P.op("dve", lambda e, dst=dst, src=src: e.tensor_copy(out=dst, in_=src), reads=[("ps", bk)], writes=[("hst", tt % 2, ts, q)])
                    else:
                        P.op("act", lambda e, dst=dst, src=src: e.copy(out=dst, in_=src), reads=[("ps", bk)], writes=[("hst", tt % 2, ts, q)])
                n += 1
            P.dma("sp", "hst%d" % (tt % 2), lambda e, hs=hs, tt=tt: e.dma_start(out=hTv[:, :, tt * TT:(tt + 1) * TT], in_=hs),
                  reads=[("hst", tt % 2, ts, q) for ts in range(4) for q in range(4)], writes=[("hT", tt)])
        P.barrier()
        sb.release(m)

    def rms_rows(self, src, ss, junk, key_src, key_ss):
        P = self.P
        P.op("act", lambda e: e.activation(out=junk, in_=src, func=AF.Square, accum_out=ss), reads=[key_src], writes=[key_ss, ("junk",)])

    def phase_mem(self):
        P = self.P; sb = self.sb
        m = sb.mark()
        gbc = sb.alloc((D,), F32)
        P.dma("sp", "gbc", lambda e: e.dma_start(out=gbc, in_=self.memg.partition_broadcast(128)), writes=[("gbc",)])
        mt = [sb.alloc((D,), F32) for _ in range(2)]
        mn = [sb.alloc((D,), BF16) for _ in range(2)]
        junk = sb.alloc((D,), BF16)
        ss = sb.alloc((4,), F32)
        for mc in range(2):
            t = mt[mc]
            P.dma("sp", "mt%d" % mc, lambda e, t=t, mc=mc: e.dma_start(out=t, in_=self.mem[mc * 128:(mc + 1) * 128, :]), writes=[("mt", mc)])
            s1 = ss[:, mc * 2:mc * 2 + 1]
            s2 = ss[:, mc * 2 + 1:mc * 2 + 2]
            P.op("act", lambda e, t=t, s1=s1: e.activation(out=junk, in_=t, func=AF.Square, accum_out=s1), reads=[("mt", mc)], writes=[("ss", mc), ("junk",)])
            P.op("act", lambda e, s1=s1, s2=s2: e.activation(out=s2, in_=s1, func=AF.Sqrt, bias=self.eps_ap, scale=1.0 / D),
                 reads=[("ss", mc), ("cbf", 4)], writes=[("ss2", mc)])
            P.op("dve", lambda e, s2=s2: e.reciprocal(out=s2, in_=s2), reads=[("ss2", mc)], writes=[("ss2", mc)])
            o = mn[mc]
            P.op("dve", lambda e, o=o, t=t, s2=s2: e.scalar_tensor_tensor(out=o, in0=t, scalar=s2, in1=gbc, op0=ALU.mult, op1=ALU.mult),
                 reads=[("mt", mc), ("ss2", mc), ("gbc",)], writes=[("mn", mc)])
            for q in range(4):
                bk = (mc * 4 + q) % 8
                bank = self.bank[bk].bitcast(BF16)
                for r in range(4):
                    kc = q * 4 + r
                    P.op("pe", lambda e, bank=bank, r=r, o=o, kc=kc: e.transpose(out=bank[:, r * 128:(r + 1) * 128], in_=o[:, kc * 128:(kc + 1) * 128], identity=self.ident_bf),
                         reads=[("mn", mc), ("cbf", 0)], writes=[("ps", bk)])
                dst = self.mem_nT[:, q * 4:(q + 1) * 4, mc * 128:(mc + 1) * 128]
                src = bank[:, 0:512].rearrange("p (a b) -> p a b", a=4)
                P.op("dve", lambda e, dst=dst, src=src: e.tensor_copy(out=dst, in_=src), reads=[("ps", bk)], writes=[("memnT",)])
        P.barrier()
        sb.release(m)

    def phase_kv(self, li):
        P = self.P; sb = self.sb
        m = sb.mark()
        wb = [sb.alloc((NKC, 512), BF16) for _ in range(2)]
        wv = self.w_kv[li].rearrange("(kc p) n -> p kc n", p=128)
        n = 0
        for blk in range(8):
            w = wb[blk % 2]
            P.dma("pool", "wkv%d" % (blk % 2), lambda e, w=w, blk=blk: e.dma_start(out=w, in_=wv[:, :, blk * 512:(blk + 1) * 512]),
                  writes=[("wkv", blk % 2)])
            if blk < 4:
                for sub in range(4):
                    bk = n % 8; n += 1
                    bank = self.bank[bk]
                    for kc in range(NKC):
                        P.op("pe", lambda e, bank=bank, w=w, kc=kc, sub=sub: e.matmul(bank[:, 0:MEM], w[:, kc, sub * 128:(sub + 1) * 128], self.mem_nT[:, kc, :], start=(kc == 0), stop=(kc == NKC - 1)),
                             reads=[("wkv", blk % 2), ("memnT",)], writes=[("ps", bk)])
                    dst = self.KT[:, blk * 4 + sub, :]
                    P.op("act", lambda e, dst=dst, bank=bank: e.copy(out=dst, in_=bank[:, 0:MEM]), reads=[("ps", bk)], writes=[("KT",)])
            else:
                for mc in range(2):
                    bk = n % 8; n += 1
                    bank = self.bank[bk]
                    for kc in range(NKC):
                        P.op("pe", lambda e, bank=bank, w=w, kc=kc, mc=mc: e.matmul(bank, self.mem_nT[:, kc, mc * 128:(mc + 1) * 128], w[:, kc, :], start=(kc == 0), stop=(kc == NKC - 1)),
                             reads=[("wkv", blk % 2), ("memnT",)], writes=[("ps", bk)])
                    dst = self.V[:, mc, (blk - 4) * 512:(blk - 3) * 512]
                    P.op("dve", lambda e, dst=dst, bank=bank: e.tensor_copy(out=dst, in_=bank), reads=[("ps", bk)], writes=[("V",)])
        P.barrier()
        sb.release(m)

    def phase_inproj(self, li, W, ncols):
        P = self.P; sb = self.sb; S = self.S; NT = self.NT
        m = sb.mark()
        xnT = sb.alloc((NKC, S), BF16)
        rstd = sb.alloc((S,), F32)
        self.rstd_all = rstd
        m2 = sb.mark()
        hb = [sb.alloc((4, TT), F32) for _ in range(2)]
        sq = [sb.alloc((TT,), BF16) for _ in range(2)]
        hTv = self.hT.rearrange("(kc p) s -> p kc s", p=128)
        lg = self.lng_sb
        n = 0
        for tt in range(NT):
            bk = tt % 2
            bank = self.bank[bk]
            for q in range(4):
                h = hb[n % 2]
                P.dma("sp", "hb%d" % (n % 2), lambda e, h=h, q=q, tt=tt: e.dma_start(out=h, in_=hTv[:, q * 4:(q + 1) * 4, tt * TT:(tt + 1) * TT]),
                      reads=[("hT", tt)], writes=[("hb", n % 2)])
                for r in range(4):
                    kc = q * 4 + r
                    s = sq[kc % 2]
                    P.op("act", lambda e, s=s, h=h, r=r: e.activation(out=s, in_=h[:, r, :], func=AF.Square), reads=[("hb", n % 2)], writes=[("sq", kc % 2)])
                    P.op("pe", lambda e, bank=bank, s=s, kc=kc: e.matmul(bank, self.onesD_bf, s, start=(kc == 0), stop=(kc == NKC - 1)),
                         reads=[("sq", kc % 2), ("cbf", 1)], writes=[("ps", bk)])
                    dst = xnT[:, kc, tt * TT:(tt + 1) * TT]
                    g = lg[:, li * NKC + kc:li * NKC + kc + 1]
                    P.op("dve", lambda e, dst=dst, h=h, r=r, g=g: e.tensor_scalar(out=dst, in0=h[:, r, :], scalar1=g, scalar2=None, op0=ALU.mult),
                         reads=[("hb", n % 2), ("lng",)], writes=[("xnT", tt)])
                n += 1
            rs = rstd[:, tt * TT:(tt + 1) * TT]
            P.op("act", lambda e, rs=rs, bank=bank: e.activation(out=rs, in_=bank, func=AF.Sqrt, bias=self.eps_ap, scale=1.0),
                 reads=[("ps", bk), ("cbf", 4)], writes=[("rstd", tt)])
            P.op("dve", lambda e, rs=rs: e.reciprocal(out=rs, in_=rs), reads=[("rstd", tt)], writes=[("rstd", tt)])
        P.barrier()
        sb.release(m2)
        CB = 256
        Wv = W.rearrange("(kc p) n -> p kc n", p=128)
        nfull = (ncols // CB)
        m3 = sb.mark()
        if ncols % CB:
            rem = ncols - nfull * CB
            wt = sb.alloc((NKC, rem), BF16)
            P.dma("pool", "wtail", lambda e: e.dma_start(out=wt, in_=Wv[:, :, nfull * CB:ncols]), writes=[("wtail",)])
            abs_ = [sb.alloc((rem,), F32) for _ in range(2)]
            rt = [sb.alloc((1,), F32) for _ in range(2)]
            for t in range(S // 128):
                bk = 4 + (t % 2) * 2
                bank = self.bank[bk]
                bank2 = self.bank[bk + 1]
                for kc in range(NKC):
                    P.op("pe", lambda e, bank=bank, kc=kc, t=t: e.matmul(bank[:, 0:rem], xnT[:, kc, t * 128:(t + 1) * 128], wt[:, kc, :], start=(kc == 0), stop=(kc == NKC - 1)),
                         reads=[("wtail",), ("xnT", t // 4)], writes=[("ps", bk)])
                P.op("pe", lambda e, bank2=bank2, t=t: e.transpose(out=bank2[:, 0:128], in_=rstd[:, t * 128:(t + 1) * 128], identity=self.ident),
                     reads=[("rstd", t // 4), ("cst",)], writes=[("ps", bk + 1)])
                r1 = rt[t % 2]
                P.op("act", lambda e, r1=r1, bank2=bank2: e.copy(out=r1, in_=bank2[:, 0:1]), reads=[("ps", bk + 1)], writes=[("rt", t % 2)])
                a = abs_[t % 2]
                P.op("dve", lambda e, a=a, bank=bank, r1=r1: e.tensor_scalar(out=a, in0=bank[:, 0:rem], scalar1=r1, scalar2=None, op0=ALU.mult),
                     reads=[("ps", bk), ("rt", t % 2)], writes=[("abs", t % 2)])
                P.dma("sp", "abs%d" % (t % 2), lambda e, a=a, t=t: e.dma_start(out=self.abt[t * 128:(t + 1) * 128, :], in_=a),
                      reads=[("abs", t % 2)], writes=[("abt", t)])
            P.barrier()
        sb.release(m3)
        wb = [sb.alloc((NKC, CB), BF16) for _ in range(2)]
        ost = [sb.alloc((S,), BF16) for _ in range(2)]
        n = 0
        no = 0
        for cb in range(nfull):
            w = wb[cb % 2]
            P.dma("pool", "win%d" % (cb % 2), lambda e, w=w, cb=cb: e.dma_start(out=w, in_=Wv[:, :, cb * CB:(cb + 1) * CB]), writes=[("win", cb % 2)])
            for sub in range(CB // 128):
                o = ost[no % 2]
                for tt in range(NT):
                    bk = n % 4; n += 1
                    bank = self.bank[bk]
                    for kc in range(NKC):
                        P.op("pe", lambda e, bank=bank, w=w, kc=kc, sub=sub, tt=tt: e.matmul(bank, w[:, kc, sub * 128:(sub + 1) * 128], xnT[:, kc, tt * TT:(tt + 1) * TT], start=(kc == 0), stop=(kc == NKC - 1)),
                             reads=[("win", cb % 2), ("xnT", tt)], writes=[("ps", bk)])
                    dst = o[:, tt * TT:(tt + 1) * TT]
                    rs = rstd[:, tt * TT:(tt + 1) * TT]
                    P.op("dve", lambda e, dst=dst, bank=bank, rs=rs: e.tensor_tensor(out=dst, in0=bank, in1=rs, op=ALU.mult),
                         reads=[("ps", bk), ("rstd", tt)], writes=[("ost", no % 2, tt)])
                row0 = cb * CB + sub * 128
                P.dma("sp", "ost%d" % (no % 2), lambda e, o=o, row0=row0: e.dma_start(out=self.proj[row0:row0 + 128, :], in_=o),
                      reads=[("ost", no % 2, tt) for tt in range(NT)], writes=[("proj", row0 // 128)])
                no += 1
        P.barrier()
        sb.release(m)

    def attention_tile(self, li, tt, qrow0, sz_loader, mix):
        P = self.P; sb = self.sb
        scale = 512.0 ** -0.5
        projv = self.proj.rearrange("(c p) s -> p c s", p=128)
        c0 = qrow0 // 128
        n = self.att_n
        for h in range(4):
            if h % 2 == 0:
                mix.begin(4 + h // 2)
            qs = self.att_nq % 2; self.att_nq += 1
            qT = self.att_q[qs]
            P.dma("sp", "attq%d" % qs, lambda e, qT=qT, h=h: e.dma_start(out=qT, in_=projv[:, c0 + h * 4:c0 + h * 4 + 4, tt * TT:(tt + 1) * TT]),
                  reads=[("proj", c0 + h * 4 + i) for i in range(4)], writes=[("attq", qs)])
            PT = self.att_PT[h % 2]
            for ts in range(4):
                bk = 4 + n % 2
                bank = self.bank[bk]
                for dc in range(4):
                    P.op("pe", lambda e, bank=bank, h=h, dc=dc, ts=ts, qT=qT: e.matmul(bank[:, 0:MEM], qT[:, dc, ts * 128:(ts + 1) * 128], self.KT[:, h * 4 + dc, :], start=(dc == 0), stop=(dc == 3)),
                         reads=[("attq", qs), ("KT",)], writes=[("ps", bk)])
                i = n % 2
                mx = self.att_small[:, i * 4 + 0:i * 4 + 1]
                sm = self.att_small[:, i * 4 + 1:i * 4 + 2]
                rs = self.att_small[:, i * 4 + 2:i * 4 + 3]
                pe_ = self.att_p[i]
                pn = self.att_pn[i]
                P.op("dve", lambda e, mx=mx, bank=bank: e.tensor_reduce(out=mx, in_=bank[:, 0:MEM], axis=AX.X, op=ALU.max), reads=[("ps", bk)], writes=[("amx", i)])
                P.op("dve", lambda e, mx=mx: e.tensor_scalar(out=mx, in0=mx, scalar1=-scale, scalar2=None, op0=ALU.mult), reads=[("amx", i)], writes=[("amx", i)])
                P.op("act", lambda e, pe_=pe_, bank=bank, mx=mx, sm=sm: e.activation(out=pe_, in_=bank[:, 0:MEM], func=AF.Exp, bias=mx, scale=scale, accum_out=sm),
                     reads=[("ps", bk), ("amx", i)], writes=[("ap", i), ("asm", i)])
                P.op("dve", lambda e, rs=rs, sm=sm: e.reciprocal(out=rs, in_=sm), reads=[("asm", i)], writes=[("ars", i)])
                P.op("dve", lambda e, pn=pn, pe_=pe_, rs=rs: e.tensor_scalar(out=pn, in0=pe_, scalar1=rs, scalar2=None, op0=ALU.mult),
                     reads=[("ap", i), ("ars", i)], writes=[("apn", i)])
                bk2 = 6 + n % 2
                bank2 = self.bank[bk2].bitcast(BF16)
                for mc in range(2):
                    P.op("pe", lambda e, bank2=bank2, pn=pn, mc=mc: e.transpose(out=bank2[:, mc * 128:(mc + 1) * 128], in_=pn[:, mc * 128:(mc + 1) * 128], identity=self.ident_bf),
                         reads=[("apn", i), ("cbf", 0)], writes=[("ps", bk2)])
                dst = PT[:, :, ts * 128:(ts + 1) * 128]
                src = bank2[:, 0:256].rearrange("p (a b) -> p a b", a=2)
                P.op("act", lambda e, dst=dst, src=src: e.copy(out=dst, in_=src), reads=[("ps", bk2)], writes=[("aPT", h % 2, ts)])
                n += 1
            for dc in range(4):
                bk = (h * 4 + dc) % 4
                bank = self.bank[bk]
                for mc in range(2):
                    P.op("pe", lambda e, bank=bank, mc=mc, h=h, dc=dc, PT=PT: e.matmul(bank, self.V[:, mc, (h * 4 + dc) * 128:(h * 4 + dc + 1) * 128], PT[:, mc, :], start=(mc == 0), stop=(mc == 1)),
                         reads=[("V",)] + [("aPT", h % 2, ts) for ts in range(4)], writes=[("ps", bk)])
                c = 32 + h * 4 + dc
                szc, szkey = sz_loader(c)
                dst, dkey = mix.dst(c)
                P.op("dve", lambda e, dst=dst, bank=bank, szc=szc: e.tensor_tensor(out=dst, in0=bank, in1=szc, op=ALU.mult),
                     reads=[("ps", bk), szkey], writes=[dkey])
            if h % 2 == 1:
                mix.flush(tt)
        self.att_n = n

    def alloc_attention(self):
        sb = self.sb
        self.att_q = [sb.alloc((4, TT), BF16) for _ in range(2)]
        self.att_PT = [sb.alloc((2, TT), BF16) for _ in range(2)]
        self.att_p = [sb.alloc((MEM,), F32) for _ in range(2)]
        self.att_pn = [sb.alloc((MEM,), BF16) for _ in range(2)]
        self.att_small = sb.alloc((8,), F32)
        self.att_n = 0
        self.att_nq = 0

    def phase_pool_mix(self, li, j):
        P = self.P; sb = self.sb; NT = self.NT
        m = sb.mark()
        projv = self.proj.rearrange("(c p) s -> p c s", p=128)
        psc = sb.alloc((32,), F32)
        P.dma("sp", "psc", lambda e: e.dma_start(out=psc, in_=self.pool_scale[:, j * 32:(j + 1) * 32]), writes=[("psc",)])
        mix = MixStage(self)
        ub = [sb.alloc((8, TT + 16), BF16) for _ in range(2)]
        db = [sb.alloc((8, TT), BF16) for _ in range(2)]
        t1 = sb.alloc((4, TT + 16), F32)
        t2 = sb.alloc((4, TT + 16), F32)
        mp = [sb.alloc((8, 1024), BF16) for _ in range(2)]
        zb = [sb.alloc((8, TT), BF16) for _ in range(2)]
        self.alloc_attention()
        cst = self.cst
        zc0 = BR // 128
        ng = 0
        nz = 0
        nb = 0
        for tt in range(NT):
            zstate = {}

            def load_z(grp, tt=tt):
                nonlocal nz
                zt = zb[nz % 2]
                slot = nz % 2
                nz += 1
                P.dma("sp", "zb%d" % slot, lambda e, zt=zt, grp=grp: e.dma_start(out=zt, in_=projv[:, zc0 + grp * 8:zc0 + grp * 8 + 8, tt * TT:(tt + 1) * TT]),
                      reads=[("proj", zc0 + grp * 8 + i) for i in range(8)], writes=[("zb", slot)])
                P.op("act", lambda e, zt=zt: e.activation(out=zt, in_=zt, func=AF.Silu), reads=[("zb", slot)], writes=[("zb", slot)])
                zstate[grp] = (zt, slot)

            def sz_loader(c):
                grp = c // 8
                if grp not in zstate:
                    load_z(grp)
                zt, slot = zstate[grp]
                return zt[:, c % 8, :], ("zb", slot)

            for g in range(4):
                w = 2 ** (g + 1)
                u = ub[ng % 2]; d = db[ng % 2]; us = ng % 2
                mw = mp[ng % 2]
                ng += 1
                mix.begin(g)
                P.dma("pool", "mp%d" % us, lambda e, mw=mw, g=g: e.dma_start(out=mw, in_=self.pool_maps[j, g].rearrange("(cc p) n -> p cc n", p=128)),
                      writes=[("mp", us)])
                if tt == 0:
                    P.op("pool", lambda e, u=u: e.memset(u[:, :, 0:16], 0.0), writes=[("ub", us)])
                    P.dma("sp", "ub%d" % us, lambda e, u=u, g=g: e.dma_start(out=u[:, :, 16:16 + TT], in_=projv[:, g * 8:g * 8 + 8, 0:TT]),
                          reads=[("proj", g * 8 + i) for i in range(8)], writes=[("ub", us)])
                else:
                    P.dma("sp", "ub%d" % us, lambda e, u=u, g=g, tt=tt: e.dma_start(out=u, in_=projv[:, g * 8:g * 8 + 8, tt * TT - 16:(tt + 1) * TT]),
                          reads=[("proj", g * 8 + i) for i in range(8)], writes=[("ub", us)])
                L = TT + 16
                for hf in range(2):
                    uh = u[:, hf * 4:(hf + 1) * 4, :]
                    dh = d[:, hf * 4:(hf + 1) * 4, :]
                    cur = uh
                    sh = 1
                    k = 0
                    while sh < w:
                        dstt = t1 if k % 2 == 0 else t2
                        P.op("dve", lambda e, dstt=dstt, cur=cur, sh=sh: e.tensor_tensor(out=dstt[:, :, 2 * sh - 1:L], in0=cur[:, :, 2 * sh - 1:L], in1=cur[:, :, sh - 1:L - sh], op=ALU.add),
                             reads=[("ub", us), ("t1",), ("t2",)], writes=[("t1",) if k % 2 == 0 else ("t2",)])
                        cur = dstt
                        sh *= 2
                        k += 1
                    P.op("dve", lambda e, dh=dh, cur=cur, uh=uh, w=w: e.scalar_tensor_tensor(out=dh, in0=cur[:, :, 16:L], scalar=1.0 / w, in1=uh[:, :, 16:L], op0=ALU.mult, op1=ALU.subtract),
                         reads=[("ub", us), ("t1",), ("t2",)], writes=[("db", us)])
                    if tt == 0:
                        inv = cst[:, C_INV + g * 128:C_INV + g * 128 + 64].rearrange("p (a b) -> p a b", a=4)
                        other = t2 if cur is t1 else t1
                        P.op("dve", lambda e, other=other, cur=cur, inv=inv: e.tensor_tensor(out=other[:, :, 0:16], in0=cur[:, :, 16:32], in1=inv, op=ALU.mult),
                             reads=[("t1",), ("t2",), ("cst",)], writes=[("t1",), ("t2",)])
                        P.op("dve", lambda e, dh=dh, other=other, uh=uh: e.tensor_tensor(out=dh[:, :, 0:16], in0=other[:, :, 0:16], in1=uh[:, :, 16:32], op=ALU.subtract),
                             reads=[("t1",), ("t2",), ("ub", us)], writes=[("db", us)])
                for oc in range(8):
                    bk = nb % 4; nb += 1
                    bank = self.bank[bk]
                    for cc in range(8):
                        P.op("pe", lambda e, bank=bank, mw=mw, cc=cc, oc=oc, d=d: e.matmul(bank, mw[:, cc, oc * 128:(oc + 1) * 128], d[:, cc, :], start=(cc == 0), stop=(cc == 7)),
                             reads=[("mp", us), ("db", us)], writes=[("ps", bk)])
                    c = g * 8 + oc
                    szc, szkey = sz_loader(c)
                    dst, dkey = mix.dst(c)
                    sc = psc[:, c:c + 1]
                    P.op("dve", lambda e, dst=dst, bank=bank, sc=sc, szc=szc: e.scalar_tensor_tensor(out=dst, in0=bank, scalar=sc, in1=szc, op0=ALU.mult, op1=ALU.mult),
                         reads=[("ps", bk), ("psc",), szkey], writes=[dkey])
                mix.flush(tt)
            self.attention_tile(li, tt, BR + MIX, sz_loader, mix)
        P.barrier()
        sb.release(m)

    def phase_delta_mix(self, li, j):
        P = self.P; sb = self.sb; S = self.S; NT = self.NT
        NC = S // 128
        m = sb.mark()
        projv = self.proj.rearrange("(c p) s -> p c s", p=128)
        cst = self.cst
        ONES = cst[:, C_ONE:C_ONE + 128]
        IDN = cst[:, C_ID:C_ID + 128]
        MUP = cst[:, C_MUP:C_MUP + 128]
        MLO = cst[:, C_MLO:C_MLO + 128]
        one_col = cst[:, C_ONE:C_ONE + 1]
        G = NC * 32
        gc = sb.alloc((G,), F32); ngc = sb.alloc((G,), F32); beta = sb.alloc((G,), F32); nbeta = sb.alloc((G,), F32)
        bg = sb.alloc((G,), F32); kd = sb.alloc((G,), F32); egl = sb.alloc((G,), F32)
        cw = sb.alloc((256,), F32)
        dng = sb.alloc((1,), F32)
        P.dma("sp", "cw", lambda e: e.dma_start(out=cw, in_=self.conv_w[:, j * 256:(j + 1) * 256]), writes=[("cw",)])
        P.dma("sp", "dng", lambda e: e.dma_start(out=dng, in_=self.dn_g[j]), writes=[("dng",)])
        mg = sb.mark()
        AB = sb.alloc((NC, 64), F32)
        alr = sb.alloc((G,), F32); dtr = sb.alloc((G,), F32)
        x1 = sb.alloc((G,), F32); x2 = sb.alloc((G,), F32); x3 = sb.alloc((G,), F32)
        P.dma("sp", "AB", lambda e: e.dma_start(out=AB, in_=self.abt.rearrange("(c p) f -> p c f", p=128)), writes=[("AB",)])
        P.dma("sp", "alr", lambda e: e.dma_start(out=alr, in_=self.a_log_rep[j]), writes=[("alr",)])
        P.dma("sp", "dtr", lambda e: e.dma_start(out=dtr, in_=self.dt_bias_rep[j]), writes=[("dtr",)])
        v3 = lambda t: t.rearrange("p (c h) -> p c h", c=NC)
        bl = AB[:, :, 0:32]; al = AB[:, :, 32:64]
        P.op("act", lambda e: e.activation(out=v3(beta), in_=bl, func=AF.Sigmoid), reads=[("AB",)], writes=[("beta",)])
        P.op("dve", lambda e: e.tensor_scalar(out=nbeta, in0=beta, scalar1=-1.0, scalar2=None, op0=ALU.mult), reads=[("beta",)], writes=[("nbeta",)])
        P.op("dve", lambda e: e.tensor_tensor(out=v3(x1), in0=al, in1=v3(dtr), op=ALU.add), reads=[("AB",), ("dtr",)], writes=[("x1",)])
        P.op("act", lambda e: e.activation(out=x2, in_=x1, func=AF.Abs), reads=[("x1",)], writes=[("x2",)])
        P.op("act", lambda e: e.activation(out=x2, in_=x2, func=AF.Exp, scale=-1.0), reads=[("x2",)], writes=[("x2",)])
        P.op("act", lambda e: e.activation(out=x2, in_=x2, func=AF.Ln, bias=one_col, scale=1.0), reads=[("x2",), ("cst",)], writes=[("x2",)])
        P.op("dve", lambda e: e.tensor_scalar(out=x1, in0=x1, scalar1=0.0, scalar2=None, op0=ALU.max), reads=[("x1",)], writes=[("x1",)])
        P.op("dve", lambda e: e.tensor_tensor(out=x1, in0=x1, in1=x2, op=ALU.add), reads=[("x1",), ("x2",)], writes=[("x1",)])
        P.op("act", lambda e: e.activation(out=x3, in_=alr, func=AF.Exp), reads=[("alr",)], writes=[("x3",)])
        P.op("dve", lambda e: e.scalar_tensor_tensor(out=x1, in0=x1, scalar=-1.0, in1=x3, op0=ALU.mult, op1=ALU.mult), reads=[("x1",), ("x3",)], writes=[("x1",)])
        GS = min(512, G)
        for hf in range(G // GS):
            sl = slice(hf * GS, (hf + 1) * GS)
            bk = hf % 2
            P.op("pe", lambda e, sl=sl, bk=bk: e.matmul(self.bank[bk][:, 0:GS], cst[:, C_UT:C_UT + 128], x1[:, sl], start=True, stop=True), reads=[("x1",), ("cst",)], writes=[("ps", bk)])
            P.op("act", lambda e, sl=sl, bk=bk: e.copy(out=gc[:, sl], in_=self.bank[bk][:, 0:GS]), reads=[("ps", bk)], writes=[("gc",)])
            P.op("pe", lambda e, sl=sl, bk=bk: e.matmul(self.bank[bk + 2][:, 0:GS], ONES, x1[:, sl], start=True, stop=True), reads=[("x1",), ("cst",)], writes=[("ps", bk + 2)])
            P.op("act", lambda e, sl=sl, bk=bk: e.copy(out=x2[:, sl], in_=self.bank[bk + 2][:, 0:GS]), reads=[("ps", bk + 2)], writes=[("x2",)])
        P.op("dve", lambda e: e.tensor_scalar(out=ngc, in0=gc, scalar1=-1.0, scalar2=None, op0=ALU.mult), reads=[("gc",)], writes=[("ngc",)])
        P.op("act", lambda e: e.activation(out=egl, in_=x2, func=AF.Exp), reads=[("x2",)], writes=[("egl",)])
        P.op("dve", lambda e: e.tensor_tensor(out=x2, in0=x2, in1=gc, op=ALU.subtract), reads=[("x2",), ("gc",)], writes=[("x2",)])
        P.op("act", lambda e: e.activation(out=kd, in_=x2, func=AF.Exp), reads=[("x2",)], writes=[("kd",)])
        P.op("act", lambda e: e.activation(out=x3, in_=gc, func=AF.Exp), reads=[("gc",)], writes=[("x3",)])
        P.op("dve", lambda e: e.tensor_tensor(out=bg, in0=x3, in1=nbeta, op=ALU.mult), reads=[("x3",), ("nbeta",)], writes=[("bg",)])
        P.barrier()
        sb.release(mg)
        if os.environ.get("KDBG", "") == "gates":
            sb.release(m)
            return
        raw = [sb.alloc((S + 3,), BF16)] * 2
        acc = sb.alloc((S,), F32)
        sqb = [sb.alloc((TT,), BF16) for _ in range(2)]
        rnb = [sb.alloc((TT,), F32) for _ in range(2)]
        qT = sb.alloc((S,), BF16); kT = sb.alloc((S,), BF16)
        vT = [sb.alloc((S,), BF16) for _ in range(2)]
        oT = [sb.alloc((S,), F32) for _ in range(2)]
        ost = [sb.alloc((S,), BF16) for _ in range(2)]
        zb = [sb.alloc((TT,), BF16) for _ in range(2)]
        S32 = [sb.alloc((128,), F32) for _ in range(2)]
        Sbf = [sb.alloc((128,), BF16) for _ in range(2)]
        def pe2(shape, dt_):
            return [[sb.alloc(shape, dt_) for _ in range(2)] for _ in range(2)]
        KQ = [sb.alloc((256,), F32) for _ in range(2)]
        ktok = [sb.alloc((128,), BF16) for _ in range(2)]
        Vb = pe2((128,), F32); Kd = pe2((128,), BF16); Dbf = pe2((128,), F32)
        Slo = [sb.alloc((128,), BF16) for _ in range(2)]
        nbg = bg
        Rd = pe2((128,), F32); E2 = pe2((128,), F32); E2T = pe2((128,), F32); EG = pe2((128,), F32)
        Mt = pe2((128,), F32); MP = pe2((256,), F32); TTt = pe2((128,), F32)
        QKd = pe2((128,), BF16); QgT = pe2((128,), BF16)
        vnew = pe2((128,), BF16)
        bank = self.bank
        nraw = 0
        nsq = 0
        nz = 0
        qscale = 128.0 ** -0.5
        for g in range(16):
            for which in range(4):
                ci = (g, 16 + g, 32 + 2 * g, 33 + 2 * g)[which]
                r = raw[0]; rs_ = 0; nraw += 1
                P.op("pool", lambda e, r=r: e.memset(r[:, 0:3], 0.0), writes=[("raw", rs_)])
                P.dma("sp", "raw%d" % rs_, lambda e, r=r, ci=ci: e.dma_start(out=r[:, 3:3 + S], in_=projv[:, ci, :]), reads=[("proj", ci)], writes=[("raw", rs_)])
                w = [cw[:, ci * 4 + t:ci * 4 + t + 1] for t in range(4)]
                P.op("dve", lambda e, r=r, w=w: e.tensor_scalar(out=acc, in0=r[:, 0:S], scalar1=w[0], scalar2=None, op0=ALU.mult), reads=[("raw", rs_), ("cw",)], writes=[("acc",)])
                for t in range(1, 4):
                    P.op("dve", lambda e, r=r, w=w, t=t: e.scalar_tensor_tensor(out=acc, in0=r[:, t:t + S], scalar=w[t], in1=acc, op0=ALU.mult, op1=ALU.add),
                         reads=[("raw", rs_), ("cw",), ("acc",)], writes=[("acc",)])
                if which >= 2:
                    dstv = vT[which - 2]
                    P.op("act", lambda e, dstv=dstv: e.activation(out=dstv, in_=acc, func=AF.Silu), reads=[("acc",)], writes=[("vT", which - 2)])
                else:
                    dstq = qT if which == 0 else kT
                    dk_ = ("qT",) if which == 0 else ("kT",)
                    P.op("act", lambda e: e.activation(out=acc, in_=acc, func=AF.Silu), reads=[("acc",)], writes=[("acc",)])
                    for tt in range(NT):
                        i = nsq % 2; nsq += 1
                        sl = slice(tt * TT, (tt + 1) * TT)
                        bk = 6 + i
                        P.op("act", lambda e, i=i, sl=sl: e.activation(out=sqb[i], in_=acc[:, sl], func=AF.Square), reads=[("acc",)], writes=[("sqb", i)])
                        P.op("pe", lambda e, i=i, bk=bk: e.matmul(bank[bk], self.ones_bf, sqb[i], start=True, stop=True), reads=[("sqb", i), ("cbf", 2)], writes=[("ps", bk, 0), ("ps", bk, 1), ("ps", bk, 2), ("ps", bk, 3)])
                        P.op("act", lambda e, i=i, bk=bk: e.activation(out=rnb[i], in_=bank[bk], func=AF.Sqrt, bias=self.eps_ap, scale=1.0),
                             reads=[("ps", bk, 0), ("ps", bk, 1), ("ps", bk, 2), ("ps", bk, 3), ("cbf", 4)], writes=[("rnb", i)])
                        P.op("dve", lambda e, i=i: e.reciprocal(out=rnb[i], in_=rnb[i]), reads=[("rnb", i)], writes=[("rnb", i)])
                        if which == 0:
                            P.op("dve", lambda e, i=i, sl=sl, dstq=dstq: e.scalar_tensor_tensor(out=dstq[:, sl], in0=acc[:, sl], scalar=qscale, in1=rnb[i], op0=ALU.mult, op1=ALU.mult),
                                 reads=[("acc",), ("rnb", i)], writes=[dk_])
                        else:
                            P.op("dve", lambda e, i=i, sl=sl, dstq=dstq: e.tensor_tensor(out=dstq[:, sl], in0=acc[:, sl], in1=rnb[i], op=ALU.mult),
                                 reads=[("acc",), ("rnb", i)], writes=[dk_])
            if os.environ.get("KDBG", "") == "prep":
                P.barrier(); sb.release(m)
                return
            for e_ in range(2):
                P.op("pool", lambda e, e_=e_: e.memset(S32[e_], 0.0), writes=[("S32", e_)])
                P.op("pool", lambda e, e_=e_: e.memset(Sbf[e_], 0.0), writes=[("Sbf", e_)])
                P.op("pool", lambda e, e_=e_: e.memset(Slo[e_], 0.0), writes=[("Slo", e_)])
            bA = bank[0]; bB = bank[1].bitcast(BF16)
            kA = ("ps", 0); kB = ("ps", 1)
            bC = [bank[2 + e_] for e_ in range(2)]; kC = [("ps", 2 + e_) for e_ in range(2)]
            bE = [bank[4 + e_] for e_ in range(2)]; kE = [("ps", 4 + e_) for e_ in range(2)]
            bS = bank[6]; kS = ("ps", 6)
            bT = bank[7]; kT_ = ("ps", 7)

            def pre_stages(c):
                p = c % 2
                cs = slice(c * 128, (c + 1) * 128)
                A0 = p * 256
                B0 = p * 512
                col = [c * 32 + 2 * g + e_ for e_ in range(2)]
                st = []

                def s0():
                    P.op("pe", lambda e: e.matmul(bA[:, A0:A0 + 128], kT[:, cs], kT[:, cs], start=True, stop=True), reads=[("kT",)], writes=[kA])
                    P.op("pe", lambda e: e.matmul(bA[:, A0 + 128:A0 + 256], kT[:, cs], qT[:, cs], start=True, stop=True), reads=[("kT",), ("qT",)], writes=[kA])
                    P.op("pe", lambda e: e.transpose(out=bB[:, B0:B0 + 128], in_=kT[:, cs], identity=self.ident_bf), reads=[("kT",), ("cbf", 0)], writes=[kB])
                    for e_ in range(2):
                        P.op("pe", lambda e, e_=e_: e.transpose(out=bB[:, B0 + 128 * (1 + e_):B0 + 128 * (2 + e_)], in_=vT[e_][:, cs], identity=self.ident_bf),
                             reads=[("vT", e_), ("cbf", 0)], writes=[kB])
                    for e_ in range(2):
                        cl = col[e_]
                        P.op("dve", lambda e, e_=e_, cl=cl: e.tensor_scalar(out=Rd[p][e_], in0=IDN, scalar1=gc[:, cl:cl + 1], scalar2=None, op0=ALU.mult),
                             reads=[("cst",), ("gc",)], writes=[("Rd", p, e_)])
                    P.op("act", lambda e: e.copy(out=KQ[p], in_=bA[:, A0:A0 + 256]), reads=[kA], writes=[("KQ", p)])
                    P.op("dve", lambda e: e.tensor_copy(out=ktok[p], in_=bB[:, B0:B0 + 128]), reads=[kB], writes=[("ktok", p)])
                st.append(s0)

                def s1():
                    for e_ in range(2):
                        P.op("pe", lambda e, e_=e_: e.matmul(bC[e_][:, 0:128], ONES, Rd[p][e_], start=True, stop=True), reads=[("cst",), ("Rd", p, e_)], writes=[kC[e_]])
                        P.op("pe", lambda e, e_=e_: e.matmul(bC[e_][:, 128:256], ONES, Rd[p][e_], start=True, stop=False), reads=[("cst",), ("Rd", p, e_)], writes=[kC[e_]])
                        P.op("pe", lambda e, e_=e_: e.matmul(bC[e_][:, 128:256], IDN, MUP, start=False, stop=True), reads=[("cst",)], writes=[kC[e_]])
                        P.op("pe", lambda e, e_=e_: e.matmul(bC[e_][:, 256:384], ONES, Rd[p][e_], start=True, stop=False), reads=[("cst",), ("Rd", p, e_)], writes=[kC[e_]])
                        P.op("pe", lambda e, e_=e_: e.matmul(bC[e_][:, 256:384], IDN, MLO, start=False, stop=True), reads=[("cst",)], writes=[kC[e_]])
                    for e_ in range(2):
                        cl = col[e_]
                        vsrc = bB[:, B0 + 128 * (1 + e_):B0 + 128 * (2 + e_)]
                        P.op("dve", lambda e, e_=e_, cl=cl, vsrc=vsrc: e.tensor_scalar(out=Vb[p][e_], in0=vsrc, scalar1=beta[:, cl:cl + 1], scalar2=None, op0=ALU.mult),
                             reads=[kB, ("beta",)], writes=[("Vb", p, e_)])
                        P.op("pool", lambda e, e_=e_, cl=cl: e.tensor_scalar(out=Kd[p][e_], in0=ktok[p], scalar1=kd[:, cl:cl + 1], scalar2=None, op0=ALU.mult),
                             reads=[("ktok", p), ("kd",)], writes=[("Kd", p, e_)])
                        P.op("act", lambda e, e_=e_, cl=cl: e.activation(out=E2T[p][e_], in_=bC[e_][:, 256:384], func=AF.Exp, bias=gc[:, cl:cl + 1], scale=-1.0),
                             reads=[kC[e_], ("gc",)], writes=[("E2T", p, e_)])
                    for e_ in range(2):
                        cl = col[e_]
                        P.op("act", lambda e, e_=e_, cl=cl: e.activation(out=E2[p][e_], in_=bC[e_][:, 128:256], func=AF.Exp, bias=ngc[:, cl:cl + 1], scale=1.0),
                             reads=[kC[e_], ("ngc",)], writes=[("E2", p, e_)])
                        P.op("act", lambda e, e_=e_: e.activation(out=EG[p][e_], in_=bC[e_][:, 0:128], func=AF.Exp), reads=[kC[e_]], writes=[("EG", p, e_)])
                        P.op("dve", lambda e, e_=e_, cl=cl: e.scalar_tensor_tensor(out=Mt[p][e_], in0=KQ[p][:, 0:128], scalar=nbeta[:, cl:cl + 1], in1=E2T[p][e_], op0=ALU.mult, op1=ALU.mult),
                             reads=[("KQ", p), ("nbeta",), ("E2T", p, e_)], writes=[("Mt", p, e_)])
                st.append(s1)

                def s2():
                    for e_ in range(2):
                        P.op("pe", lambda e, e_=e_: e.matmul(bE[e_][:, 256:384], Mt[p][e_], IDN, start=True, stop=True), reads=[("Mt", p, e_), ("cst",)], writes=[kE[e_]])
                    for e_ in range(2):
                        P.op("pool", lambda e, e_=e_: e.tensor_tensor(out=QKd[p][e_], in0=KQ[p][:, 128:256], in1=E2[p][e_], op=ALU.mult),
                             reads=[("KQ", p), ("E2", p, e_)], writes=[("QKd", p, e_)])
                        P.op("pool", lambda e, e_=e_: e.tensor_tensor(out=QgT[p][e_], in0=qT[:, cs], in1=EG[p][e_], op=ALU.mult),
                             reads=[("qT",), ("EG", p, e_)], writes=[("QgT", p, e_)])
                    for e_ in range(2):
                        P.op("dve", lambda e, e_=e_: e.tensor_copy(out=MP[p][e_][:, 0:128], in_=bE[e_][:, 256:384]), reads=[kE[e_]], writes=[("M", p, e_)])
                        P.op("dve", lambda e, e_=e_: e.tensor_tensor(out=MP[p][e_][:, 128:256], in0=MP[p][e_][:, 0:128], in1=IDN, op=ALU.add),
                             reads=[("M", p, e_), ("cst",)], writes=[("Pm", p, e_)])
                st.append(s2)

                def s3():
                    for e_ in range(2):
                        P.op("pe", lambda e, e_=e_: e.matmul(bE[e_][:, 0:128], Mt[p][e_], MP[p][e_][:, 0:128], start=True, stop=True), reads=[("Mt", p, e_), ("M", p, e_)], writes=[kE[e_]])
                        P.op("pe", lambda e, e_=e_: e.matmul(bC[e_][:, 384:512], MP[p][e_][:, 0:128], Mt[p][e_], start=True, stop=True), reads=[("Mt", p, e_), ("M", p, e_)], writes=[kC[e_]])
                    for e_ in range(2):
                        P.op("dve", lambda e, e_=e_: e.tensor_copy(out=MP[p][e_][:, 0:128], in_=bE[e_][:, 0:128]), reads=[kE[e_]], writes=[("M", p, e_)])
                        P.op("act", lambda e, e_=e_: e.copy(out=Mt[p][e_], in_=bC[e_][:, 384:512]), reads=[kC[e_]], writes=[("Mt", p, e_)])
                st.append(s3)

                def mk_level(k):
                    def lv():
                        for e_ in range(2):
                            if k < 6:
                                P.op("pe", lambda e, e_=e_: e.matmul(bE[e_][:, 0:256], Mt[p][e_], MP[p][e_], start=True, stop=True),
                                     reads=[("Mt", p, e_), ("M", p, e_), ("Pm", p, e_)], writes=[kE[e_]])
                                P.op("pe", lambda e, e_=e_: e.matmul(bC[e_][:, 384:512], MP[p][e_][:, 0:128], Mt[p][e_], start=True, stop=True),
                                     reads=[("Mt", p, e_), ("M", p, e_)], writes=[kC[e_]])
                            else:
                                P.op("pe", lambda e, e_=e_: e.matmul(bE[e_][:, 128:256], Mt[p][e_], MP[p][e_][:, 128:256], start=True, stop=True),
                                     reads=[("Mt", p, e_), ("Pm", p, e_)], writes=[kE[e_]])
                        for e_ in range(2):
                            if k < 5:
                                P.op("dve", lambda e, e_=e_: e.tensor_copy(out=MP[p][e_][:, 0:128], in_=bE[e_][:, 0:128]), reads=[kE[e_]], writes=[("M", p, e_)])
                            if k < 6:
                                P.op("dve", lambda e, e_=e_: e.tensor_tensor(out=MP[p][e_][:, 128:256], in0=bE[e_][:, 128:256], in1=MP[p][e_][:, 128:256], op=ALU.add),
                                     reads=[kE[e_], ("Pm", p, e_)], writes=[("Pm", p, e_)])
                                P.op("act", lambda e, e_=e_: e.copy(out=Mt[p][e_], in_=bC[e_][:, 384:512]), reads=[kC[e_]], writes=[("Mt", p, e_)])
                            else:
                                P.op("dve", lambda e, e_=e_: e.tensor_tensor(out=TTt[p][e_], in0=bE[e_][:, 128:256], in1=MP[p][e_][:, 128:256], op=ALU.add),
                                     reads=[kE[e_], ("Pm", p, e_)], writes=[("TT", p, e_)])
                    return lv
                for k in range(1, 7):
                    st.append(mk_level(k))
                return st

            def scan_stages(c):
                p = c % 2
                cs = slice(c * 128, (c + 1) * 128)
                col = [c * 32 + 2 * g + e_ for e_ in range(2)]
                st = []

                def c1():
                    for e_ in range(2):
                        r0 = e_ * 128
                        P.op("pe", lambda e, e_=e_, r0=r0: e.matmul(bS[:, r0:r0 + 128], kT[:, cs], Sbf[e_], start=True, stop=False), reads=[("kT",), ("Sbf", e_)], writes=[kS])
                        P.op("pe", lambda e, e_=e_, r0=r0: e.matmul(bS[:, r0:r0 + 128], kT[:, cs], Slo[e_], start=False, stop=True), reads=[("kT",), ("Slo", e_)], writes=[kS])
                    for e_ in range(2):
                        cl = col[e_]; r0 = e_ * 128
                        P.op("dve", lambda e, e_=e_, cl=cl, r0=r0: e.scalar_tensor_tensor(out=Dbf[p][e_], in0=bS[:, r0:r0 + 128], scalar=nbg[:, cl:cl + 1], in1=Vb[p][e_], op0=ALU.mult, op1=ALU.add),
                             reads=[kS, ("bg",), ("Vb", p, e_)], writes=[("Dbf", p, e_)])
                st.append(c1)

                def c2():
                    for e_ in range(2):
                        r0 = e_ * 128
                        P.op("pe", lambda e, e_=e_, r0=r0: e.matmul(bT[:, r0:r0 + 128], TTt[p][e_], Dbf[p][e_], start=True, stop=True), reads=[("TT", p, e_), ("Dbf", p, e_)], writes=[kT_])
                    for e_ in range(2):
                        r0 = e_ * 128
                        P.op("act", lambda e, e_=e_, r0=r0: e.copy(out=vnew[p][e_], in_=bT[:, r0:r0 + 128]), reads=[kT_], writes=[("vnew", p, e_)])
                st.append(c2)

                def c3():
                    for e_ in range(2):
                        r0 = 256 + e_ * 128
                        P.op("pe", lambda e, e_=e_, r0=r0: e.matmul(bT[:, r0:r0 + 128], Sbf[e_], QgT[p][e_], start=True, stop=False), reads=[("Sbf", e_), ("QgT", p, e_)], writes=[kT_])
                        P.op("pe", lambda e, e_=e_, r0=r0: e.matmul(bT[:, r0:r0 + 128], vnew[p][e_], QKd[p][e_], start=False, stop=True), reads=[("vnew", p, e_), ("QKd", p, e_)], writes=[kT_])
                        P.op("pe", lambda e, e_=e_, r0=r0: e.matmul(bS[:, r0:r0 + 128], Kd[p][e_], vnew[p][e_], start=True, stop=True), reads=[("Kd", p, e_), ("vnew", p, e_)], writes=[kS])
                    for e_ in range(2):
                        cl = col[e_]; r0 = 256 + e_ * 128
                        P.op("act", lambda e, e_=e_, r0=r0: e.copy(out=oT[e_][:, cs], in_=bT[:, r0:r0 + 128]), reads=[kT_], writes=[("oT", e_, c)])
                        P.op("dve", lambda e, e_=e_, cl=cl, r0=r0: e.scalar_tensor_tensor(out=S32[e_], in0=S32[e_], scalar=egl[:, cl:cl + 1], in1=bS[:, r0:r0 + 128], op0=ALU.mult, op1=ALU.add),
                             reads=[("S32", e_), ("egl",), kS], writes=[("S32", e_)])
                st.append(c3)

                def c4():
                    for e_ in range(2):
                        P.op("act", lambda e, e_=e_: e.copy(out=Sbf[e_], in_=S32[e_]), reads=[("S32", e_)], writes=[("Sbf", e_)])
                        P.op("pool", lambda e, e_=e_: e.tensor_tensor(out=Slo[e_], in0=S32[e_], in1=Sbf[e_], op=ALU.subtract), reads=[("S32", e_), ("Sbf", e_)], writes=[("Slo", e_)])
                st.append(c4)
                return st

            for f in pre_stages(0):
                f()
            for c in range(NC):
                A = pre_stages(c + 1) if c + 1 < NC else []
                B = scan_stages(c)
                bi = 0
                for ai, f in enumerate(A):
                    f()
                    if ai % 2 == 1 and bi < len(B):
                        B[bi](); bi += 1
                while bi < len(B):
                    B[bi](); bi += 1
            if os.environ.get("KDBG", "") in ("chunk", "c1", "c2", "c3"):
                P.barrier(); sb.release(m)
                return
            for e_ in range(2):
                hv = 2 * g + e_
                o = oT[e_]
                okeys = [("oT", e_, c) for c in range(NC)]
                for tt in range(NT):
                    i = nsq % 2; nsq += 1
                    sl = slice(tt * TT, (tt + 1) * TT)
                    bk = 6 + i
                    kk = [("ps", bk, r) for r in range(4)]
                    tkeys = [("oT", e_, c) for c in range(tt * 4, tt * 4 + 4)]
                    P.op("act", lambda e, i=i, sl=sl, o=o: e.activation(out=sqb[i], in_=o[:, sl], func=AF.Square), reads=tkeys, writes=[("sqb", i)])
                    P.op("pe", lambda e, i=i, bk=bk: e.matmul(bank[bk], self.ones128_bf, sqb[i], start=True, stop=True), reads=[("sqb", i), ("cbf", 3)], writes=kk)
                    P.op("act", lambda e, i=i, bk=bk: e.activation(out=rnb[i], in_=bank[bk], func=AF.Sqrt, bias=self.eps_ap, scale=1.0), reads=kk + [("cbf", 4)], writes=[("rnb", i)])
                    P.op("dve", lambda e, i=i: e.reciprocal(out=rnb[i], in_=rnb[i]), reads=[("rnb", i)], writes=[("rnb", i)])
                    zi = nz % 2; nz += 1
                    zc = 64 + hv
                    P.dma("sp", "dzb%d" % zi, lambda e, zi=zi, zc=zc, sl=sl: e.dma_start(out=zb[zi], in_=projv[:, zc, sl]), reads=[("proj", zc)], writes=[("dzb", zi)])
                    P.op("act", lambda e, zi=zi: e.activation(out=zb[zi], in_=zb[zi], func=AF.Silu), reads=[("dzb", zi)], writes=[("dzb", zi)])
                    P.op("dve", lambda e, i=i, sl=sl, o=o: e.tensor_tensor(out=o[:, sl], in0=o[:, sl], in1=rnb[i], op=ALU.mult), reads=tkeys + [("rnb", i)], writes=tkeys)
                    P.op("dve", lambda e, zi=zi, sl=sl, o=o, e_=e_: e.scalar_tensor_tensor(out=ost[e_][:, sl], in0=o[:, sl], scalar=dng[:, 0:1], in1=zb[zi], op0=ALU.mult, op1=ALU.mult),
                         reads=tkeys + [("dng",), ("dzb", zi)], writes=[("ost", e_, tt)])
                P.dma("sp", "dost%d" % e_, lambda e, e_=e_, hv=hv: e.dma_start(out=self.mixed[hv * 128:(hv + 1) * 128, :], in_=ost[e_]),
                      reads=[("ost", e_, tt) for tt in range(NT)], writes=[("mixed", hv)])
        P.barrier()
        sb.release(m)
        m = sb.mark()
        mix = MixStage(self)
        zb2 = [sb.alloc((8, TT), BF16) for _ in range(2)]
        self.alloc_attention()
        nz = 0
        for tt in range(NT):
            zstate = {}

            def sz_loader(c, tt=tt, zstate=zstate):
                nonlocal nz
                grp = c // 8
                if grp not in zstate:
                    zt = zb2[nz % 2]; slot = nz % 2; nz += 1
                    P.dma("sp", "zb%d" % slot, lambda e, zt=zt, grp=grp: e.dma_start(out=zt, in_=projv[:, 64 + grp * 8:64 + grp * 8 + 8, tt * TT:(tt + 1) * TT]),
                          reads=[("proj", 64 + grp * 8 + i) for i in range(8)], writes=[("zb", slot)])
                    P.op("act", lambda e, zt=zt: e.activation(out=zt, in_=zt, func=AF.Silu), reads=[("zb", slot)], writes=[("zb", slot)])
                    zstate[grp] = (zt, slot)
                zt, slot = zstate[grp]
                return zt[:, c % 8, :], ("zb", slot)

            self.attention_tile(li, tt, 8192 + MIX, sz_loader, mix)
        P.barrier()
        sb.release(m)

    def phase_outproj(self, li):
        P = self.P; sb = self.sb; NT = self.NT
        m = sb.mark()
        mixv = self.mixed.rearrange("(c p) s -> p c s", p=128)
        hTv = self.hT.rearrange("(kc p) s -> p kc s", p=128)
        Wv = self.w_out[li].rearrange("(kc p) n -> p kc n", p=128)
        NH = 2 if NT % 2 == 0 else 1
        mx = [sb.alloc((48, TT), BF16) for _ in range(2)]
        wb = [sb.alloc((8, 512), BF16) for _ in range(4)]
        hb = [sb.alloc((4, TT), F32) for _ in range(4)]
        nw = 0
        nh = 0
        for tp in range(NT // NH):
            tts = [tp * NH + i for i in range(NH)]
            for i, tt in enumerate(tts):
                P.dma("sp", "mx%d" % i, lambda e, i=i, tt=tt: e.dma_start(out=mx[i], in_=mixv[:, :, tt * TT:(tt + 1) * TT]),
                      reads=[("mixed", tt)], writes=[("mx", i)])
            for q in range(4):
                hs = []
                for i, tt in enumerate(tts):
                    hi = nh % 4; nh += 1
                    hs.append(hi)
                    P.dma("sp", "ohb%d" % hi, lambda e, hi=hi, q=q, tt=tt: e.dma_start(out=hb[hi], in_=hTv[:, q * 4:(q + 1) * 4, tt * TT:(tt + 1) * TT]),
                          reads=[("hT", tt, q)], writes=[("ohb", hi)])
                for kg in range(6):
                    w = wb[nw % 4]; ws = nw % 4; nw += 1
                    P.dma("pool", "wo%d" % ws, lambda e, w=w, kg=kg, q=q: e.dma_start(out=w, in_=Wv[:, kg * 8:(kg + 1) * 8, q * 512:(q + 1) * 512]),
                          writes=[("wo", ws)])
                    for r in range(4):
                        for k8 in range(8):
                            kc = kg * 8 + k8
                            for i in range(NH):
                                bk = i * 4 + r
                                P.op("pe", lambda e, bk=bk, w=w, k8=k8, r=r, i=i, kc=kc: e.matmul(self.bank[bk], w[:, k8, r * 128:(r + 1) * 128], mx[i][:, kc, :], start=(kc == 0), stop=(kc == 47)),
                                     reads=[("wo", ws), ("mx", i)], writes=[("ps", bk)])
                for i, tt in enumerate(tts):
                    hi = hs[i]
                    for r in range(4):
                        bk = i * 4 + r
                        P.op("dve", lambda e, hi=hi, r=r, bk=bk: e.tensor_tensor(out=hb[hi][:, r, :], in0=self.bank[bk], in1=hb[hi][:, r, :], op=ALU.add),
                             reads=[("ps", bk), ("ohb", hi)], writes=[("ohb", hi)])
                    P.dma("sp", "ohb%d" % hi, lambda e, hi=hi, q=q, tt=tt: e.dma_start(out=hTv[:, q * 4:(q + 1) * 4, tt * TT:(tt + 1) * TT], in_=hb[hi]),
                          reads=[("ohb", hi)], writes=[("hT", tt, q)])
        P.barrier()
        sb.release(m)

    def phase_final(self):
        P = self.P; sb = self.sb; S = self.S
        m = sb.mark()
        gbc = sb.alloc((D,), F32)
        P.dma("sp", "gbc", lambda e: e.dma_start(out=gbc, in_=self.fing.partition_broadcast(128)), writes=[("gbc",)])
        hTv = self.hT.rearrange("(kc p) s -> p kc s", p=128)
        hin = [sb.alloc((NKC, 128), F32) for _ in range(2)]
        xt = [sb.alloc((D,), F32) for _ in range(2)]
        ot = [sb.alloc((D,), F32) for _ in range(2)]
        junk = sb.alloc((D,), BF16)
        ss = sb.alloc((4,), F32)
        for t in range(S // 128):
            i = t % 2
            hi = hin[i]; x = xt[i]; o = ot[i]
            P.dma("sp", "hin%d" % i, lambda e, hi=hi, t=t: e.dma_start(out=hi, in_=hTv[:, :, t * 128:(t + 1) * 128]), writes=[("hin", i)])
            for q in range(4):
                bk = (t * 4 + q) % 8
                bank = self.bank[bk]
                for r in range(4):
                    kc = q * 4 + r
                    P.op("pe", lambda e, bank=bank, r=r, hi=hi, kc=kc: e.transpose(out=bank[:, r * 128:(r + 1) * 128], in_=hi[:, kc, :], identity=self.ident),
                         reads=[("hin", i), ("cst",)], writes=[("ps", bk)])
                dst = x[:, q * 512:(q + 1) * 512]
                if q % 2 == 0:
                    P.op("dve", lambda e, dst=dst, bank=bank: e.tensor_copy(out=dst, in_=bank), reads=[("ps", bk)], writes=[("xt", i, q)])
                else:
                    P.op("act", lambda e, dst=dst, bank=bank: e.copy(out=dst, in_=bank), reads=[("ps", bk)], writes=[("xt", i, q)])
            s1 = ss[:, i * 2:i * 2 + 1]
            s2 = ss[:, i * 2 + 1:i * 2 + 2]
            xk = [("xt", i, q) for q in range(4)]
            P.op("act", lambda e, x=x, s1=s1: e.activation(out=junk, in_=x, func=AF.Square, accum_out=s1), reads=xk, writes=[("fss", i), ("junk",)])
            P.op("act", lambda e, s1=s1, s2=s2: e.activation(out=s2, in_=s1, func=AF.Sqrt, bias=self.eps_ap, scale=1.0 / D),
                 reads=[("fss", i), ("cbf", 4)], writes=[("fss2", i)])
            P.op("dve", lambda e, s2=s2: e.reciprocal(out=s2, in_=s2), reads=[("fss2", i)], writes=[("fss2", i)])
            P.op("dve", lambda e, o=o, x=x, s2=s2: e.scalar_tensor_tensor(out=o, in0=x, scalar=s2, in1=gbc, op0=ALU.mult, op1=ALU.mult),
                 reads=xk + [("fss2", i), ("gbc",)], writes=[("ot", i)])
            P.dma("sp", "ot%d" % i, lambda e, o=o, t=t: e.dma_start(out=self.out[t * 128:(t + 1) * 128, :], in_=o), reads=[("ot", i)], writes=[("out", t)])
        P.barrier()
        sb.release(m)


LAYERS_FULL = [("pool", 0, 0), ("delta", 1, 0), ("pool", 2, 1), ("delta", 3, 1)]


def chunk_cols(v):
    return np.ascontiguousarray(np.asarray(v, np.float32).reshape(-1, 128).T)


def prepare_shared(inputs, layers, S):
    pool_js = sorted({j for k, l, j in layers if k == "pool"}) or [0]
    delta_js = sorted({j for k, l, j in layers if k == "delta"}) or [0]
    ls = [l for k, l, j in layers]
    f = lambda a: np.ascontiguousarray(np.asarray(a, np.float32))
    sh = {}
    sh["consts"] = make_consts()
    sh["lng"] = np.concatenate([chunk_cols(inputs["layer_norm_g"][l]) for l in ls], axis=1)
    sh["memg"] = f(inputs["mem_norm_g"])
    sh["fing"] = f(inputs["final_norm_g"])
    sh["w_in_pool"] = f(np.asarray(inputs["w_in_pool"])[pool_js])
    sh["pool_maps"] = f(np.asarray(inputs["pool_maps"])[pool_js])
    sh["pool_scale"] = np.concatenate([chunk_cols(inputs["pool_scale"][j]) for j in pool_js], axis=1)
    sh["w_in_delta"] = f(np.asarray(inputs["w_in_delta"])[delta_js])
    cw = []
    for j in delta_js:
        w = np.asarray(inputs["dn_conv_w"][j], np.float32)
        cw.append(np.ascontiguousarray(w.reshape(4, 64, 128).transpose(2, 1, 0)).reshape(128, 256))
    sh["conv_w"] = np.ascontiguousarray(np.concatenate(cw, axis=1))
    NC = S // 128
    sh["a_log_rep"] = np.ascontiguousarray(np.stack([np.tile(np.asarray(inputs["dn_a_log"][j], np.float32)[None, :], (128, NC)) for j in delta_js]))
    sh["dt_bias_rep"] = np.ascontiguousarray(np.stack([np.tile(np.asarray(inputs["dn_dt_bias"][j], np.float32)[None, :], (128, NC)) for j in delta_js]))
    sh["dn_g"] = np.ascontiguousarray(np.stack([np.asarray(inputs["dn_norm_g"][j], np.float32) for j in delta_js], axis=0)[:, :, None])
    sh["w_kv"] = f(np.asarray(inputs["w_mem_kv"])[ls])
    sh["w_out"] = f(np.asarray(inputs["w_out"])[ls])
    return sh


def remap_layers(layers):
    pool_js = sorted({j for k, l, j in layers if k == "pool"}) or [0]
    delta_js = sorted({j for k, l, j in layers if k == "delta"}) or [0]
    out = []
    for k, l, j in layers:
        out.append((k, l, pool_js.index(j) if k == "pool" else delta_js.index(j)))
    return out


def run(inputs, layers=LAYERS_FULL, n_cores=8, S=4096, trace=False):
    sh = prepare_shared(inputs, layers, S)
    b = Builder(S, remap_layers(layers))
    nc = b.build()
    x = np.asarray(inputs["x"], np.float32)
    mem = np.asarray(inputs["mem"], np.float32)
    in_maps = []
    for c in range(n_cores):
        d = dict(sh)
        d["x"] = np.ascontiguousarray(x[c, :S])
        d["mem"] = np.ascontiguousarray(mem[c])
        in_maps.append(d)
    res = run_bass_kernel_spmd(nc, in_maps, core_ids=list(range(n_cores)), trace=trace)
    out = np.stack([np.asarray(r["out"]) for r in res.results], axis=0)
    return out, res


def kernel(**inputs):
    out, _ = run(inputs)
    return out.astype(np.float32)
```
